# Optimizing a Trainium2 kernel written in Bass

```python
import math, functools
import jax, jax.numpy as jnp
from jax import lax
import numpy as np

D_MODEL = 1024
BATCH = 4
SEQ = 4096
DEPTH = 2
DEC_BATCH = 128
DEC_SEQ = 4
PAST_LEN = 8192
PAGE_SIZE = 128

D_MIX = D_MODEL
D_CONV = D_MIX // 4
CONV_WIDTH = 31
H_HGRN = 4
DK_HGRN = 128
D_HGRN = D_MIX // 2
DV_HGRN = D_HGRN // H_HGRN
HGRN_CHUNK = 64
HEAD_DIM = 64
D_ATTN = D_MIX - D_CONV - D_HGRN
H_ATTN = D_ATTN // HEAD_DIM
KV_HEADS = 2
GROUP = H_ATTN // KV_HEADS
WINDOW = 128
ATTN_BLOCK = 128
NUM_BUCKETS = 32
MAX_DISTANCE = 128
D_FF = 4 * D_MODEL
N_MOD = 6
EPS = 1e-6
IN_SIZES = (D_CONV, D_CONV, H_HGRN * DK_HGRN, H_HGRN * DK_HGRN, D_HGRN, D_HGRN,
            D_ATTN, KV_HEADS * HEAD_DIM, KV_HEADS * HEAD_DIM)
IN_WIDTH = sum(IN_SIZES)

kernel_name = "hymba_conv_hgrn2_swa_adaln_step"


def rmsnorm(x, g):
    xf = x.astype(jnp.float32)
    y = xf * lax.rsqrt(jnp.mean(xf * xf, axis=-1, keepdims=True) + EPS)
    return (y * g.astype(jnp.float32)).astype(x.dtype)


def layernorm(x, g, b):
    xf = x.astype(jnp.float32)
    mu = jnp.mean(xf, axis=-1, keepdims=True)
    xc = xf - mu
    y = xc * lax.rsqrt(jnp.mean(xc * xc, axis=-1, keepdims=True) + EPS)
    return (y * g.astype(jnp.float32) + b.astype(jnp.float32)).astype(x.dtype)


def causal_dwconv(a_prev, a, w, b):
    full = jnp.concatenate([a_prev, a], axis=1)
    out = lax.conv_general_dilated(full, w[:, None, :].astype(full.dtype), window_strides=(1,), padding='VALID',
                                   dimension_numbers=('NWC', 'WIO', 'NWC'), feature_group_count=a.shape[-1])
    return out + b, full[:, -(CONV_WIDTH - 1):]


def hgrn2_chunked(q, log_f, k, v, S0):
    B, T = q.shape[:2]
    L = min(HGRN_CHUNK, T)
    n = -(-T // L)
    pad = n * L - T

    def prep(t):
        t = jnp.pad(t, ((0, 0), (0, pad), (0, 0), (0, 0)))
        return t.reshape(B, n, L, t.shape[2], t.shape[3]).transpose(1, 0, 3, 2, 4)

    tri = jnp.tril(jnp.ones((L, L), dtype=bool))

    def step(S, inp):
        qc, lfc, kc, vc = inp
        bcum = jnp.cumsum(lfc, axis=2)
        diff = bcum[:, :, :, None, :] - bcum[:, :, None, :, :]
        decay = jnp.exp(jnp.where(tri[:, :, None], diff, -jnp.inf))
        A = jnp.einsum('bhtd,bhsd,bhtsd->bhts', qc, kc, decay)
        o = jnp.einsum('bhts,bhsv->bhtv', A, vc) + jnp.einsum('bhtd,bhdv->bhtv', qc * jnp.exp(bcum), S)
        bL = bcum[:, :, -1]
        S_new = jnp.exp(bL)[..., None] * S + jnp.einsum('bhsd,bhsv->bhdv', kc * jnp.exp(bL[:, :, None] - bcum), vc)
        return S_new, o

    S_fin, o = lax.scan(step, S0, (prep(q), prep(log_f), prep(k), prep(v)))
    o = o.transpose(1, 0, 3, 2, 4).reshape(B, n * L, q.shape[2], v.shape[3])[:, :T]
    return o, S_fin


def t5_bucket(rel):
    n = jnp.maximum(rel, 0)
    max_exact = NUM_BUCKETS // 2
    nf = jnp.maximum(n, max_exact).astype(jnp.float32)
    large = max_exact + (jnp.log(nf / max_exact) / math.log(MAX_DISTANCE / max_exact)
                         * (NUM_BUCKETS - max_exact)).astype(jnp.int32)
    large = jnp.minimum(large, NUM_BUCKETS - 1)
    return jnp.where(n < max_exact, n, large)


def band_bias(q_pos, k_pos, table):
    rel = q_pos[..., :, None] - k_pos[..., None, :]
    mask = (rel >= 0) & (rel <= WINDOW) & (k_pos[..., None, :] >= 0)
    bias = jnp.moveaxis(table.astype(jnp.float32)[t5_bucket(rel)], -1, -3)
    return jnp.where(mask[..., None, :, :], bias, -jnp.inf)


def sink_attention(q, k, v, bias, sinks):
    qg = q.reshape(q.shape[:-2] + (KV_HEADS, GROUP, HEAD_DIM))
    bias = bias.reshape(bias.shape[:-3] + (KV_HEADS, GROUP) + bias.shape[-2:])
    s = jnp.einsum('...qkgd,...skd->...kgqs', qg, k).astype(jnp.float32) * (HEAD_DIM ** -0.5) + bias
    sink = sinks.astype(jnp.float32).reshape(KV_HEADS, GROUP)[:, :, None, None]
    m = jnp.maximum(jnp.max(s, axis=-1, keepdims=True), sink)
    p = jnp.exp(s - m)
    p = p / (jnp.sum(p, axis=-1, keepdims=True) + jnp.exp(sink - m))
    o = jnp.einsum('...kgqs,...skd->...qkgd', p.astype(v.dtype), v)
    return o.reshape(q.shape[:-2] + (H_ATTN * HEAD_DIM,))


def swa_prompt(q, k, v, sinks, bias):
    B, T = q.shape[:2]
    nb = T // ATTN_BLOCK
    qb = q.reshape(B, nb, ATTN_BLOCK, H_ATTN, HEAD_DIM)
    kb = k.reshape(B, nb, ATTN_BLOCK, KV_HEADS, HEAD_DIM)
    vb = v.reshape(B, nb, ATTN_BLOCK, KV_HEADS, HEAD_DIM)
    kk = jnp.concatenate([jnp.concatenate([jnp.zeros_like(kb[:, :1]), kb[:, :-1]], axis=1), kb], axis=2)
    vv = jnp.concatenate([jnp.concatenate([jnp.zeros_like(vb[:, :1]), vb[:, :-1]], axis=1), vb], axis=2)
    o = sink_attention(qb, kk, vv, bias, sinks).reshape(B, T, D_ATTN)
    keep = min(WINDOW, T)
    return o, k[:, -keep:], v[:, -keep:]


def swa_sample(q, k, v, sinks, bias, k_buf, v_buf):
    kk = jnp.concatenate([k_buf, k], axis=1)
    vv = jnp.concatenate([v_buf, v], axis=1)
    o = sink_attention(q, kk, vv, bias, sinks)
    keep = k_buf.shape[1]
    return o, kk[:, -keep:], vv[:, -keep:]


def layer(x, c, conv_prev, S0, attn_fn, lb, w_ada, b_ada, norm_mix_g, w_in, conv_w, conv_b, conv_ln_g, conv_ln_b,
          hgrn_norm_g, sinks, w_out, norm_mlp_g, w_up, w_down):
    B, T = x.shape[:2]
    mod = jax.nn.silu(c) @ w_ada + b_ada
    sh1, sc1, g1, sh2, sc2, g2 = jnp.split(mod[:, None, :], N_MOD, axis=-1)
    h = rmsnorm(x, norm_mix_g) * (1 + sc1) + sh1
    proj = h @ w_in
    a_val, a_gate, q_h, f_h, i_h, g_h, q_a, k_a, v_a = jnp.split(proj, np.cumsum(IN_SIZES)[:-1].tolist(), axis=-1)
    a = a_val * jax.nn.sigmoid(a_gate)
    a_conv, conv_new = causal_dwconv(conv_prev, a, conv_w, conv_b)
    out_a = jax.nn.silu(layernorm(a_conv, conv_ln_g, conv_ln_b))
    lbh = lb.reshape(H_HGRN, DK_HGRN)
    f = lbh + (1 - lbh) * jax.nn.sigmoid(f_h.reshape(B, T, H_HGRN, DK_HGRN).astype(jnp.float32))
    o_b, S_new = hgrn2_chunked(q_h.reshape(B, T, H_HGRN, DK_HGRN).astype(jnp.float32), jnp.log(f), 1 - f,
                               i_h.reshape(B, T, H_HGRN, DV_HGRN).astype(jnp.float32), S0.astype(jnp.float32))
    out_b = (rmsnorm(o_b, hgrn_norm_g) * jax.nn.silu(g_h.reshape(B, T, H_HGRN, DV_HGRN).astype(jnp.float32)))
    out_b = out_b.reshape(B, T, D_HGRN).astype(x.dtype)
    out_c, k_new, v_new = attn_fn(q_a.reshape(B, T, H_ATTN, HEAD_DIM), k_a.reshape(B, T, KV_HEADS, HEAD_DIM),
                                  v_a.reshape(B, T, KV_HEADS, HEAD_DIM), sinks)
    mix = jnp.concatenate([out_a, out_b, out_c.astype(x.dtype)], axis=-1) @ w_out
    x = x + g1 * mix
    h2 = rmsnorm(x, norm_mlp_g) * (1 + sc2) + sh2
    x = x + g2 * (jnp.square(jax.nn.relu(h2 @ w_up)) @ w_down)
    return x, conv_new, S_new.astype(S0.dtype), k_new, v_new


def setup_inputs(seed: int = 0) -> dict:
    key = jax.random.key(seed)
    ks = jax.random.split(key, 26)
    f32 = jnp.float32

    def nrm(k, shape, scale=1.0):
        return scale * jax.random.normal(k, shape, f32)

    w_buf = min(WINDOW, PAST_LEN)
    return {
        "x_prompt": nrm(ks[0], (BATCH, SEQ, D_MODEL)),
        "x_sample": nrm(ks[1], (DEC_BATCH, DEC_SEQ, D_MODEL)),
        "cache_conv": nrm(ks[2], (DEPTH, DEC_BATCH, CONV_WIDTH - 1, D_CONV), 0.5),
        "state_hgrn": nrm(ks[3], (DEPTH, DEC_BATCH, H_HGRN, DK_HGRN, DV_HGRN), 0.5),
        "cache_swa_k": nrm(ks[4], (DEPTH, DEC_BATCH, w_buf, KV_HEADS, HEAD_DIM)),
        "cache_swa_v": nrm(ks[5], (DEPTH, DEC_BATCH, w_buf, KV_HEADS, HEAD_DIM)),
        "c_prompt": nrm(ks[6], (BATCH, D_MODEL)),
        "c_sample": nrm(ks[7], (DEC_BATCH, D_MODEL)),
        "rel_bias": nrm(ks[8], (NUM_BUCKETS, H_ATTN), 0.5),
        "w_ada": nrm(ks[9], (DEPTH, D_MODEL, N_MOD * D_MODEL), 0.5 * D_MODEL ** -0.5),
        "b_ada": nrm(ks[10], (DEPTH, N_MOD * D_MODEL), 0.02),
        "norm_mix_g": 1 + nrm(ks[11], (DEPTH, D_MODEL), 0.05),
        "w_in": nrm(ks[12], (DEPTH, D_MODEL, IN_WIDTH), D_MODEL ** -0.5),
        "conv_w": nrm(ks[13], (DEPTH, CONV_WIDTH, D_CONV), CONV_WIDTH ** -0.5),
        "conv_b": nrm(ks[14], (DEPTH, D_CONV), 0.02),
        "conv_ln_g": 1 + nrm(ks[15], (DEPTH, D_CONV), 0.05),
        "conv_ln_b": nrm(ks[16], (DEPTH, D_CONV), 0.02),
        "hgrn_lb": nrm(ks[17], (DEPTH, H_HGRN * DK_HGRN), 0.5),
        "hgrn_norm_g": 1 + nrm(ks[18], (DEPTH, DV_HGRN), 0.05),
        "attn_sinks": nrm(ks[19], (DEPTH, H_ATTN), 0.5),
        "w_out": nrm(ks[20], (DEPTH, D_MIX, D_MODEL), D_MIX ** -0.5),
        "norm_mlp_g": 1 + nrm(ks[21], (DEPTH, D_MODEL), 0.05),
        "w_up": nrm(ks[22], (DEPTH, D_MODEL, D_FF), D_MODEL ** -0.5),
        "w_down": nrm(ks[23], (DEPTH, D_FF, D_MODEL), D_FF ** -0.5),
        "final_g": 1 + nrm(ks[24], (D_MODEL,), 0.05),
    }


def reference(x_prompt, x_sample, cache_conv, state_hgrn, cache_swa_k, cache_swa_v, c_prompt, c_sample, rel_bias,
              w_ada, b_ada, norm_mix_g, w_in, conv_w, conv_b, conv_ln_g, conv_ln_b, hgrn_lb, hgrn_norm_g,
              attn_sinks, w_out, norm_mlp_g, w_up, w_down, final_g):
    Bp, Tp = x_prompt.shape[:2]
    Bs, Ts = x_sample.shape[:2]
    w_buf = cache_swa_k.shape[2]
    lbs = jnp.cumsum(jax.nn.softmax(hgrn_lb.astype(jnp.float32), axis=0), axis=0)
    lbs = lbs - lbs[0]
    nb = Tp // ATTN_BLOCK
    qpos_p = jnp.arange(Tp, dtype=jnp.int32).reshape(nb, ATTN_BLOCK)
    kpos_p = (jnp.arange(nb, dtype=jnp.int32) * ATTN_BLOCK)[:, None] - ATTN_BLOCK + jnp.arange(2 * ATTN_BLOCK, dtype=jnp.int32)[None]
    bias_p = band_bias(qpos_p, kpos_p, rel_bias)
    qpos_s = PAST_LEN + jnp.arange(Ts, dtype=jnp.int32)
    kpos_s = PAST_LEN - w_buf + jnp.arange(w_buf + Ts, dtype=jnp.int32)
    bias_s = band_bias(qpos_s, kpos_s, rel_bias)

    xp, xs = x_prompt, x_sample
    conv_p, conv_s, hg_p, hg_s, kp_l, ks_l, vp_l, vs_l = [], [], [], [], [], [], [], []
    for l in range(DEPTH):
        params = (w_ada[l], b_ada[l], norm_mix_g[l], w_in[l], conv_w[l], conv_b[l], conv_ln_g[l], conv_ln_b[l],
                  hgrn_norm_g[l], attn_sinks[l], w_out[l], norm_mlp_g[l], w_up[l], w_down[l])
        xp, cp, sp, kp, vp = layer(xp, c_prompt, jnp.zeros((Bp, CONV_WIDTH - 1, D_CONV), xp.dtype),
                                   jnp.zeros((Bp, H_HGRN, DK_HGRN, DV_HGRN), state_hgrn.dtype),
                                   functools.partial(swa_prompt, bias=bias_p), lbs[l], *params)
        xs, cs, ss, ksn, vsn = layer(xs, c_sample, cache_conv[l], state_hgrn[l],
                                     functools.partial(swa_sample, bias=bias_s, k_buf=cache_swa_k[l], v_buf=cache_swa_v[l]),
                                     lbs[l], *params)
        conv_p.append(cp); conv_s.append(cs); hg_p.append(sp); hg_s.append(ss)
        kp_l.append(kp); ks_l.append(ksn); vp_l.append(vp); vs_l.append(vsn)
    y_prompt = rmsnorm(xp, final_g)
    y_sample = rmsnorm(xs, final_g)
    return (y_prompt, y_sample, jnp.stack(conv_p), jnp.stack(conv_s), jnp.stack(hg_p), jnp.stack(hg_s),
            jnp.stack(kp_l), jnp.stack(ks_l), jnp.stack(vp_l), jnp.stack(vs_l))
```

```python
import contextlib
import numpy as np
import ml_dtypes
import concourse.bass as bass
import concourse.mybir as mybir
from concourse.bass_utils import run_bass_kernel_spmd

F32 = mybir.dt.float32
BF16 = mybir.dt.bfloat16
AF = mybir.ActivationFunctionType
ALU = mybir.AluOpType
AX = mybir.AxisListType

NCORES = 8
D = 1024
SEQ = 4096
NSS = 16
TS = 4
NT = 512
DEPTH = 2
EPS = 1e-6
EPOCH = 1000
HGRN_ON = True


class Op:
    __slots__ = ("eng", "fn", "reads", "writes", "dma", "waits", "sig", "seq", "idx", "grp")

    def __init__(self, eng, fn, reads, writes, dma):
        self.eng, self.fn, self.reads, self.writes, self.dma = eng, fn, reads, writes, dma
        self.waits = []
        self.sig = False
        self.seq = -1
        self.grp = None


class Prog:
    ENGS = ("pe", "act", "dve", "pool", "sp")

    def __init__(self):
        self.ops = []

    def add(self, eng, fn, reads=(), writes=(), dma=None):
        op = Op(eng, fn, tuple(reads), tuple(writes), dma)
        op.idx = len(self.ops)
        self.ops.append(op)
        return op

    def schedule(self):
        last_w = {}
        readers = {}
        for op in self.ops:
            deps = {}

            def need(p, kind):
                if p.dma is not None:
                    if op.dma == p.dma and kind == "WAW" and op.grp == p.grp:
                        return
                    deps[p.idx] = p
                    return
                if p.eng == op.eng:
                    if op.dma is None:
                        if op.eng == "pe":
                            return
                deps[p.idx] = p

            for r in op.reads:
                p = last_w.get(r)
                if p is not None:
                    need(p, "RAW")
                if isinstance(r, tuple) and r[0] == "ps":
                    for p in readers.get(r, ()):
                        if p.eng != op.eng:
                            need(p, "RAR")
            for w in op.writes:
                p = last_w.get(w)
                if p is not None:
                    need(p, "WAW")
                for p in readers.get(w, ()):
                    need(p, "WAR")
            best = {}
            for p in deps.values():
                if p.dma is not None:
                    best[("dma", p.idx)] = p
                else:
                    q = best.get(p.eng)
                    if q is None or q.idx < p.idx:
                        best[p.eng] = p
            op.waits = list(best.values())
            for p in op.waits:
                if p.dma is None:
                    p.sig = True
            for w in op.writes:
                last_w[w] = op
                readers[w] = []
            for r in op.reads:
                readers.setdefault(r, []).append(op)
        cnt = {e: 0 for e in self.ENGS}
        for op in self.ops:
            if op.sig:
                op.seq = cnt[op.eng]
                cnt[op.eng] += 1
        self.sigcount = cnt


def emit_program(nc, prog, final_streams):
    prog.schedule()
    with contextlib.ExitStack() as es:
        esem = {}
        for e in Prog.ENGS:
            n = (prog.sigcount[e] + EPOCH - 1) // EPOCH
            esem[e] = [es.enter_context(nc.semaphore(f"p_{e}{i}")) for i in range(n)]
        streams = {}
        for op in prog.ops:
            if op.dma is not None:
                st = streams.get(op.dma)
                if st is None:
                    st = [es.enter_context(nc.semaphore(f"d_{op.dma}")), 0]
                    streams[op.dma] = st
                st[1] += 16
                op.seq = st[1]
        setup_total = streams["setup"][1] if "setup" in streams else 0
        block = es.enter_context(nc.Block())
        by_eng = {e: [o for o in prog.ops if o.eng == e] for e in Prog.ENGS}

        def run(engh, ename):
            known = {}
            for op in by_eng[ename]:
                for p in op.waits:
                    if p.dma is not None:
                        sem = streams[p.dma][0]
                        val = setup_total if p.dma == "setup" else p.seq
                    else:
                        sem = esem[p.eng][p.seq // EPOCH]
                        val = p.seq % EPOCH + 1
                    k = id(sem)
                    if known.get(k, 0) >= val:
                        continue
                    known[k] = val
                    engh.wait_ge(sem, val)
                ins = op.fn(engh)
                if op.dma is not None:
                    ins.then_inc(streams[op.dma][0], 16)
                elif op.sig:
                    ins.then_inc(esem[ename][op.seq // EPOCH], 1)
            if ename == "sp":
                for s in final_streams:
                    if s in streams:
                        engh.wait_ge(streams[s][0], streams[s][1])

        @block.tensor
        def _(t):
            run(t, "pe")

        @block.scalar
        def _(a):
            run(a, "act")

        @block.vector
        def _(v):
            run(v, "dve")

        @block.gpsimd
        def _(g):
            run(g, "pool")

        @block.sync
        def _(s):
            run(s, "sp")


def _t5_bucket(rel):
    rel = np.asarray(rel)
    n = np.maximum(rel, 0)
    nf = np.maximum(n, 16).astype(np.float32)
    large = 16 + (np.log(nf / np.float32(16)) / np.float32(np.log(128 / 16)) * np.float32(16)).astype(np.int32)
    large = np.minimum(large, 31)
    return np.where(n < 16, n, large)


def _static_tables():
    t = {}
    t["ident"] = np.eye(128, dtype=np.float32)
    k = np.arange(128)[:, None]
    q = np.arange(128)[None, :]
    oh = np.zeros((128, 2, 32, 128), np.float32)
    rel_prev = q + 128 - k
    rel_own = q - k
    for w, rel in ((0, rel_prev), (1, rel_own)):
        valid = (rel >= 0) & (rel <= 128)
        b = _t5_bucket(rel)
        for bb in range(32):
            oh[:, w, bb, :] = (valid & (b == bb))
    t["ohp"] = oh.astype(ml_dtypes.bfloat16)
    j = np.arange(128)[:, None]
    tt = np.arange(TS)[None, :]
    rel = tt + 128 - j
    valid = (rel >= 0) & (rel <= 128)
    b = _t5_bucket(rel)
    ohs = np.zeros((128, 32, TS), np.float32)
    for bb in range(32):
        ohs[:, bb, :] = valid & (b == bb)
    t["ohsc"] = ohs.astype(ml_dtypes.bfloat16)
    n = NSS * TS
    kk = np.arange(n)[:, None]
    qq = np.arange(n)[None, :]
    same = (kk // TS) == (qq // TS)
    rel = (qq % TS) - (kk % TS)
    ohn = np.zeros((n, TS, n), np.float32)
    for bb in range(TS):
        ohn[:, bb, :] = same & (rel == bb)
    t["ohsn"] = ohn.astype(ml_dtypes.bfloat16)
    s = np.arange(128)[:, None]
    u = np.arange(128)[None, :]
    t["mAp"] = (((s // 32) == (u // 32)) & (s <= u)).astype(np.float32)
    s = np.arange(n)[:, None]
    u = np.arange(n)[None, :]
    t["mAs"] = (((s // TS) == (u // TS)) & (s <= u)).astype(np.float32)
    rp = np.ones((128, NT), np.float32)
    rp[:, ::32] = 0
    t["rstp"] = rp
    rs = np.ones((128, n), np.float32)
    rs[:, ::TS] = 0
    t["rsts"] = rs
    t["cmask"] = ((np.arange(128)[:, None] // 32) == np.arange(4)[None, :]).astype(np.float32)
    t["seqoh"] = ((np.arange(n)[:, None] // TS) == np.arange(NSS)[None, :]).astype(np.float32)
    return t


VROWS_PER_LAYER = 48 + 8 + 8 + 2 + 2 + 2 + 4 + 1 + 62
V_BADA, V_GMIX, V_GMLP, V_CB, V_LNG, V_LNB, V_LB, V_HG, V_CW = 0, 48, 56, 64, 66, 68, 70, 74, 75
V_FINAL = 2 * VROWS_PER_LAYER
NVROWS = V_FINAL + 8


def bcast(ap, n):
    return bass.AP(tensor=ap.tensor, offset=ap.offset, ap=[list(x) for x in ap.ap] + [[0, n]])


def bcast_mid(ap, n):
    a = [list(x) for x in ap.ap]
    return bass.AP(tensor=ap.tensor, offset=ap.offset, ap=[a[0], [0, n]] + a[1:])


class Builder:
    def __init__(self, stage=99):
        self.stage = stage
        self.nc = bass.Bass("TRN2", target_bir_lowering=False)
        self.P = Prog()
        self.es = contextlib.ExitStack()
        self.ps_rr = 0

    def dram_in(self, name, shape, dt=F32):
        return self.nc.dram_tensor(name, list(shape), dt, kind="ExternalInput").ap()

    def dram_out(self, name, shape, dt=F32):
        return self.nc.dram_tensor(name, list(shape), dt, kind="ExternalOutput").ap()

    def sb(self, name, shape, dt=F32):
        return self.es.enter_context(self.nc.sbuf_tensor(name, list(shape), dt))

    def psum(self, name, shape, dt=F32):
        return self.es.enter_context(self.nc.psum_tensor(name, list(shape), dt))

    def ring_ps(self, nring=4):
        i = self.ps_rr % nring
        self.ps_rr += 1
        return i

    def act(self, out, in_, func, reads, writes, bias=None, scale=None):
        kw = {}
        if bias is not None:
            kw["bias"] = bias
        if scale is not None:
            kw["scale"] = scale
        self.P.add("act", lambda e: e.activation(out=out, in_=in_, func=func, **kw), reads, writes)

    def tt(self, out, a, b, op, reads, writes, eng="dve"):
        self.P.add(eng, lambda e: e.tensor_tensor(out=out, in0=a, in1=b, op=op), reads, writes)

    def ts(self, out, a, s1, op0, reads, writes, s2=None, op1=None, eng="dve"):
        if op1 is None:
            self.P.add(eng, lambda e: e.tensor_scalar(out=out, in0=a, scalar1=s1, scalar2=None, op0=op0), reads, writes)
        else:
            self.P.add(eng, lambda e: e.tensor_scalar(out=out, in0=a, scalar1=s1, scalar2=s2, op0=op0, op1=op1), reads, writes)

    def stt(self, out, a, s, b, op0, op1, reads, writes, eng="dve"):
        self.P.add(eng, lambda e: e.scalar_tensor_tensor(out=out, in0=a, scalar=s, in1=b, op0=op0, op1=op1), reads, writes)

    def copy(self, out, in_, reads, writes, eng="dve"):
        if eng == "act":
            self.P.add("act", lambda e: e.copy(out=out, in_=in_), reads, writes)
        else:
            self.P.add(eng, lambda e: e.tensor_copy(out=out, in_=in_), reads, writes)

    def dma(self, eng, out, in_, stream, reads, writes, grp=None, **kw):
        op = self.P.add(eng, lambda e: e.dma_start(out=out, in_=in_, **kw), reads, writes, dma=stream)
        if grp is None and stream == "cin":
            grp = ("cin", op.idx)
        op.grp = grp

    def mm_group(self, outs_ins, reads, writes):
        lst = list(outs_ins)
        import os, traceback
        dbg = os.environ.get("DEBUGMM")
        where = traceback.extract_stack()[-2].lineno if dbg else None

        def fn(e):
            ins = None
            for tup in lst:
                (o, l, r, st, sp) = tup[:5]
                if len(tup) > 5:
                    ins = e.matmul(o, lhsT=l, rhs=r, start=st, stop=sp, tile_position=tup[5])
                else:
                    ins = e.matmul(o, lhsT=l, rhs=r, start=st, stop=sp)
                if dbg:
                    print("MMDBG", where, str(ins)[:50].replace("\n"," "))
            return ins
        self.P.add("pe", fn, reads, writes)

    def transpose_f32(self, out, in_, kparts, reads, writes):
        idn = self.ident[0:kparts, 0:kparts]
        self.P.add("pe", lambda e: e.matmul(out, lhsT=in_, rhs=idn, start=True, stop=True), reads + ["ident"], writes)

    def build(self):
        nc, P = self.nc, self.P
        n = NSS * TS
        d = {}
        d["xp"] = self.dram_in("xp", [SEQ, D])
        d["xs"] = self.dram_in("xs", [n, D])
        d["cc"] = self.dram_in("cc", [DEPTH, NSS * 30, 256])
        d["sh"] = self.dram_in("sh", [DEPTH, NSS, 4, 128, 128])
        d["ck"] = self.dram_in("ck", [DEPTH, NSS, 128, 128])
        d["cv"] = self.dram_in("cv", [DEPTH, NSS, 128, 128])
        d["cvec"] = self.dram_in("cvec", [1 + NSS, D])
        d["relb"] = self.dram_in("relb", [128, 128])
        d["sinkc"] = self.dram_in("sinkc", [128, DEPTH * 2])
        d["vecs"] = self.dram_in("vecs", [NVROWS, 128])
        d["w_ada"] = self.dram_in("w_ada", [DEPTH, D, 6 * D])
        d["w_in"] = self.dram_in("w_in", [DEPTH, D, 3072])
        d["w_out"] = self.dram_in("w_out", [DEPTH, D, D])
        d["w_up"] = self.dram_in("w_up", [DEPTH, D, 4 * D])
        d["w_down"] = self.dram_in("w_down", [DEPTH, 4 * D, D])
        st = _static_tables()
        d["ident"] = self.dram_in("ident", [128, 128])
        d["ohp"] = self.dram_in("ohp", [128, 2 * 32 * 128], BF16)
        d["ohsc"] = self.dram_in("ohsc", [128, 32 * TS], BF16)
        d["ohsn"] = self.dram_in("ohsn", [n, TS * n], BF16)
        d["mAp"] = self.dram_in("mAp", [128, 128])
        d["mAs"] = self.dram_in("mAs", [n, n])
        d["rstp"] = self.dram_in("rstp", [128, NT])
        d["rsts"] = self.dram_in("rsts", [128, n])
        d["seqoh"] = self.dram_in("seqoh", [n, NSS])
        d["cmask"] = self.dram_in("cmask", [128, 4])
        o = {}
        o["yp"] = self.dram_out("yp", [SEQ, D])
        o["ys"] = self.dram_out("ys", [n, D])
        o["ncp"] = self.dram_out("ncp", [DEPTH, 30, 256])
        o["ncs"] = self.dram_out("ncs", [DEPTH, NSS, 30, 256])
        o["nhp"] = self.dram_out("nhp", [DEPTH, 4, 128, 128])
        o["nhs"] = self.dram_out("nhs", [DEPTH, NSS, 4, 128, 128])
        o["nkp"] = self.dram_out("nkp", [DEPTH, 128, 128])
        o["nks"] = self.dram_out("nks", [DEPTH, NSS, 128, 128])
        o["nvp"] = self.dram_out("nvp", [DEPTH, 128, 128])
        o["nvs"] = self.dram_out("nvs", [DEPTH, NSS, 128, 128])
        self.d, self.o = d, o

        self.ident = self.sb("ident_sb", [128, 128])
        self.identb = self.sb("identb", [128, 128], BF16)
        self.onesb = self.sb("onesb", [128, 128], BF16)
        self.onesLN = self.sb("onesLN", [128, 128])
        self.epsc = self.sb("epsc", [128, 1])
        self.vecT = self.sb("vecT", [128, NVROWS])
        self.NSLOT = 4
        self.wring = [self.sb(f"wr{i}", [128, 8192], BF16) for i in range(self.NSLOT)]
        self.x = self.sb("x", [128, 8, NT])
        self.h = self.sb("h", [128, 8, NT], BF16)
        self.a_ext = self.sb("a_ext", [128, DEPTH, 2, 30 + NT], BF16)
        self.a_exts = self.sb("a_exts", [128, 2, NSS, 30 + TS], BF16)
        self.kTa = self.sb("kTa", [128, DEPTH, 128 + NT], BF16)
        self.vatm = self.sb("vatm", [128, DEPTH, 5, 2, 128], BF16)
        self.onespad = self.sb("onespad", [128, 2, 128], BF16)
        self.S = self.sb("S", [128, DEPTH, 4, 128])
        self.mod = self.sb("mod", [128, DEPTH, 48, 1 + NSS])
        self.diagw = self.sb("diagw", [128, 2, 31, 128], BF16)
        self.EBp = self.sb("EBp", [128, 2, 4, 128])
        self.EBsc = self.sb("EBsc", [128, 4, TS])
        self.EBsn = self.sb("EBsn", [n, 4, n])
        self.expT = self.sb("expT", [128, 128])
        self.lbv = self.sb("lbv", [128, DEPTH, 3, 4])
        self.esink = self.sb("esink", [128, DEPTH * 2])
        self.mAp = self.sb("mAp_sb", [128, 128])
        self.mAs = self.sb("mAs_sb", [n, n])
        self.rstp = self.sb("rstp_sb", [128, NT])
        self.rsts = self.sb("rsts_sb", [128, n])
        self.seqoh = self.sb("seqoh_sb", [n, NSS])
        self.cmask = self.sb("cmask_sb", [128, 4])
        self.csc = self.sb("csc", [128, 8, 1 + NSS], BF16)
        self.chs = self.sb("chs", [128, 4, 16])
        self.wt = self.sb("wt", [128, 2, 128])
        self.ostg = self.sb("ostg", [128, DEPTH, 2, 256])
        self.ostgs = self.sb("ostgs", [64, DEPTH, 2, 256])
        self.am = self.sb("am", [128, 4, 128], BF16)
        self.NPG = 23
        self.arena = self.sb("arena", [128, self.NPG * 512])
        self.ps = [self.psum(f"ps{i}", [128, 512]) for i in range(8)]
        self.wcount = 0

        self.setup()
        import os
        ntile = int(os.environ.get("NTILE", SEQ // NT))
        if self.stage < 50:
            ntile = 1
        self.ntile = ntile
        for t in range(ntile):
            self.tile_prompt(t, last=(t == ntile - 1) and self.stage >= 50 and not os.environ.get("NOLAST"))
        if self.stage >= 60:
            self.tile_sample()
        emit_program(nc, P, ["outy0", "outy1", "outm", "outh"])
        return nc

    def pg(self, p0, npg=1):
        return [("a", p) for p in range(p0, p0 + npg)]

    def av(self, p0, npg, dt=F32):
        v = self.arena[:, p0 * 512:(p0 + npg) * 512]
        if dt == BF16:
            v = v.bitcast(BF16)
        return v

    def psk(self, i):
        return [("ps", i)]

    def wplan_init(self, plan):
        self.wplan = plan
        self.wnext = 0
        self.wuse = 0
        for _ in range(self.NSLOT):
            self.w_record()

    def w_record(self):
        if self.wnext < len(self.wplan):
            slot = self.wnext % self.NSLOT
            self.wplan[self.wnext](slot)
            self.wnext += 1

    def w_acquire(self):
        slot = self.wuse % self.NSLOT
        self.wuse += 1
        return slot

    def w_release(self):
        self.w_record()

    def wload(self, slot, dst_sl, src, part=None):
        t = self.wring[slot]
        self.dma("pool", dst_sl(t), src, f"w{slot}", [], [("w", slot)], grp=self.wnext)

    def plan_weights(self, ntiles_total):
        d = self.d
        plan = []
        for l in range(DEPTH):
            wa = d["w_ada"][l].rearrange("(k p) n -> p k n", p=128)
            for j in range(6):
                def f(slot, wa=wa, j=j):
                    self.wload(slot, lambda t: t[:, :].rearrange("p (k n) -> p k n", k=8), wa[:, :, j * 1024:(j + 1) * 1024])
                plan.append(f)
        for _ in range(ntiles_total):
            for l in range(DEPTH):
                wi = d["w_in"][l].rearrange("(k p) n -> p k n", p=128)
                for j in range(2):
                    def f(slot, wi=wi, j=j):
                        self.wload(slot, lambda t: t[:, :].rearrange("p (k n) -> p k n", k=8), wi[:, :, j * 1024:(j + 1) * 1024])
                    plan.append(f)

                def f2(slot, wi=wi):
                    v = lambda t: t[:, :].rearrange("p (k n) -> p k n", k=8)
                    self.wload(slot, lambda t: v(t)[:, :, 0:512], wi[:, :, 2048:2560])
                    for pos, hh in enumerate((0, 2, 1, 3)):
                        self.wload(slot, lambda t, pos=pos: v(t)[:, :, 512 + pos * 64:512 + (pos + 1) * 64],
                                   wi[:, :, 2560 + hh * 64:2560 + (hh + 1) * 64])
                    self.wload(slot, lambda t: v(t)[:, :, 768:1024], wi[:, :, 2816:3072])
                plan.append(f2)
                wo = d["w_out"][l]

                def f3(slot, wo=wo):
                    v = lambda t: t[:, :].rearrange("p (k n) -> p k n", k=8)
                    self.wload(slot, lambda t: v(t)[:, 0:6, :], wo[0:768, :].rearrange("(k p) n -> p k n", p=128))
                    for kk, (ha, hb) in ((6, (0, 2)), (7, (1, 3))):
                        self.wload(slot, lambda t, kk=kk: v(t)[0:64, kk, :], wo[768 + ha * 64:768 + (ha + 1) * 64, :])
                        self.wload(slot, lambda t, kk=kk: v(t)[64:128, kk, :], wo[768 + hb * 64:768 + (hb + 1) * 64, :])
                plan.append(f3)
                wu = d["w_up"][l].rearrange("(k p) n -> p k n", p=128)
                for j in range(4):
                    def f(slot, wu=wu, j=j):
                        self.wload(slot, lambda t: t[:, :].rearrange("p (k n) -> p k n", k=8), wu[:, :, j * 1024:(j + 1) * 1024])
                    plan.append(f)
                wd = d["w_down"][l].rearrange("(k p) n -> p k n", p=128)
                for j in range(4):
                    def f(slot, wd=wd, j=j):
                        self.wload(slot, lambda t: t[:, :].rearrange("p (k n) -> p k n", k=32), wd[:, :, j * 256:(j + 1) * 256])
                    plan.append(f)
        return plan

    def setup(self):
        import os
        P, d = self.P, self.d
        n = NSS * TS
        A = self.arena
        sp = "sp"
        for (dst, src, key) in ((self.ident[:, :], d["ident"], "ident"), (self.mAp[:, :], d["mAp"], "mAp"),
                                (self.mAs[:, :], d["mAs"], "mAs"), (self.rstp[:, :], d["rstp"], "rstp"),
                                (self.rsts[:, :], d["rsts"], "rsts"), (self.seqoh[:, :], d["seqoh"], "seqoh"), (self.cmask[:, :], d["cmask"], "cmask"),
                                (self.expT[:, :], d["relb"], "expT"), (self.esink[:, :], d["sinkc"], "esink")):
            self.dma(sp, dst, src, "setup", [], [key])
        nblk = (NVROWS + 127) // 128
        for r in range(nblk):
            rows = min(128, NVROWS - r * 128)
            self.dma(sp, A[0:rows, r * 128:(r + 1) * 128], d["vecs"][r * 128:r * 128 + rows, :], "setup", [], self.pg(0))
        self.dma(sp, A[0:1 + NSS, 512:512 + D], d["cvec"], "setup", [], self.pg(1, 2))
        ohp = self.av(4, 8, BF16)
        self.dma(sp, ohp, d["ohp"], "setup", [], self.pg(4, 8))
        ohsc = self.av(20, 1, BF16)[:, 0:32 * TS]
        self.dma(sp, ohsc, d["ohsc"], "setup", [], self.pg(20))
        ohsn = self.av(21, 1, BF16)[0:n, 0:TS * n]
        self.dma(sp, ohsn, d["ohsn"], "setup", [], self.pg(21))
        ms = lambda ap, v, w, eng="dve": P.add(eng, lambda e: e.memset(ap, v), [], w)
        ms(self.onesb[:, :], 1.0, ["onesb"])
        ms(self.onesLN[:, :], 1.0 / 256.0, ["onesLN"])
        ms(self.epsc[:, :], EPS, ["epsc"])
        ms(self.a_ext[:, :, :, :], 0.0, ["a_ext0", "a_ext1"])
        ms(self.kTa[:, :, :], 0.0, ["kTa0", "kTa1"])
        ms(self.vatm[:, :, :, :, :], 0.0, ["vatm0", "vatm1"])
        ms(self.onespad[:, :, :], 0.0, ["onespad"])
        ms(self.onespad[:, 0, 0:64], 1.0, ["onespad"])
        ms(self.onespad[:, 1, 64:128], 1.0, ["onespad"])
        ms(self.S[:, :, :, :], 0.0, ["S0", "S1"])
        ms(self.lbv[:, :, :, :], 0.0, ["lbv"])
        self.copy(self.identb[:, :], self.ident[:, :], ["ident"], ["identb"])
        for r in range(nblk):
            rows = min(128, NVROWS - r * 128)
            self.transpose_f32(self.ps[r][:, 0:rows], A[0:rows, r * 128:(r + 1) * 128], rows, self.pg(0), self.psk(r))
            self.copy(self.vecT[:, r * 128:r * 128 + rows], self.ps[r][:, 0:rows], self.psk(r), ["vecT"])
        nb = 1 + NSS
        for k in range(8):
            self.transpose_f32(self.ps[4][:, k * nb:(k + 1) * nb], A[0:nb, 512 + k * 128:512 + (k + 1) * 128], nb,
                               self.pg(1, 2), self.psk(4))
        self.act(self.csc[:, :, :], self.ps[4][:, 0:8 * nb].rearrange("p (k c) -> p k c", k=8), AF.Silu, self.psk(4), ["csc"])
        self.act(self.expT[:, :], self.expT[:, :], AF.Exp, ["expT"], ["expT"])
        self.act(self.esink[:, :], self.esink[:, :], AF.Exp, ["esink"], ["esink"])
        ohp3 = ohp.rearrange("p (w b q) -> p w b q", w=2, b=32)
        for w in range(2):
            for hs_, hh in enumerate((0, 2, 1, 3)):
                dst = self.EBp[:, w, hs_, :]
                for b in range(32):
                    sc = self.expT[:, b * 4 + hh:b * 4 + hh + 1]
                    if b == 0:
                        self.ts(dst, ohp3[:, w, b, :], sc, ALU.mult, self.pg(4, 8) + ["expT"], ["EBp"])
                    else:
                        self.stt(dst, ohp3[:, w, b, :], sc, dst, ALU.mult, ALU.add, self.pg(4, 8) + ["expT", "EBp"], ["EBp"])
        ohsc3 = ohsc.rearrange("p (b t) -> p b t", b=32)
        ohsn3 = ohsn.rearrange("p (b t) -> p b t", b=TS)
        for hs_, hh in enumerate((0, 2, 1, 3)):
            dst = self.EBsc[:, hs_, :]
            for b in range(32):
                sc = self.expT[:, b * 4 + hh:b * 4 + hh + 1]
                if b == 0:
                    self.ts(dst, ohsc3[:, b, :], sc, ALU.mult, self.pg(20) + ["expT"], ["EBsc"])
                else:
                    self.stt(dst, ohsc3[:, b, :], sc, dst, ALU.mult, ALU.add, self.pg(20) + ["expT", "EBsc"], ["EBsc"])
            dst = self.EBsn[:, hs_, :]
            for b in range(TS):
                sc = self.expT[0:n, b * 4 + hh:b * 4 + hh + 1]
                if b == 0:
                    self.ts(dst, ohsn3[:, b, :], sc, ALU.mult, self.pg(21) + ["expT"], ["EBsn"])
                else:
                    self.stt(dst, ohsn3[:, b, :], sc, dst, ALU.mult, ALU.add, self.pg(21) + ["expT", "EBsn"], ["EBsn"])
        VR = VROWS_PER_LAYER
        c0 = self.vecT[:, V_LB:V_LB + 4]
        c1 = self.vecT[:, VR + V_LB:VR + V_LB + 4]
        self.tt(self.lbv[:, 1, 0, :], c1, c0, ALU.subtract, ["vecT"], ["lbv"])
        self.act(self.lbv[:, 1, 0, :], self.lbv[:, 1, 0, :], AF.Sigmoid, ["lbv"], ["lbv"])
        for l in range(DEPTH):
            self.ts(self.lbv[:, l, 1, :], self.lbv[:, l, 0, :], -1.0, ALU.mult, ["lbv"], ["lbv"], 1.0, ALU.add)
            self.ts(self.lbv[:, l, 2, :], self.lbv[:, l, 1, :], -1.0, ALU.mult, ["lbv"], ["lbv"])
        ntiles_total = (int(os.environ.get("NTILE", SEQ // NT)) if self.stage >= 50 else 1) + (1 if self.stage >= 60 else 0)
        self.wplan_init(self.plan_weights(ntiles_total))
        for l in range(DEPTH):
            for j6 in range(6):
                slot = self.w_acquire()
                wv = self.wring[slot][:, :].rearrange("p (k n) -> p k n", k=8)
                for jj in range(8):
                    j = j6 * 8 + jj
                    bank = 5 + (j // 24)
                    col = (j % 24) * nb
                    self.mm_group([(self.ps[bank][:, col:col + nb], wv[:, k, jj * 128:(jj + 1) * 128], self.csc[:, k, :], k == 0, k == 7)
                                   for k in range(8)], [("w", slot), "csc"], self.psk(bank))
                self.w_release()
            for half in range(2):
                bank = 5 + half
                self.tt(self.mod[:, l, half * 24:(half + 1) * 24, :],
                        self.ps[bank][:, 0:24 * nb].rearrange("p (j c) -> p j c", j=24),
                        bcast(self.vecT[:, l * VR + V_BADA + half * 24:l * VR + V_BADA + (half + 1) * 24], nb),
                        ALU.add, self.psk(bank) + ["vecT"], [("mod", l)])
            for (c0_, gcol) in ((8, V_GMIX), (32, V_GMLP)):
                m = self.mod[:, l, c0_:c0_ + 8, :]
                self.ts(m, m, 1.0, ALU.add, [("mod", l)], [("mod", l)])
                self.tt(m, m, bcast(self.vecT[:, l * VR + gcol:l * VR + gcol + 8], nb), ALU.mult, [("mod", l), "vecT"], [("mod", l)])

    def rmsnorm_mod(self, N, l, mul_c0, add_c0, cols, xkeys, grp):
        x, h = self.x, self.h
        sq = self.av(4, 4, BF16).rearrange("p (k n) -> p k n", k=8)
        for k in range(8):
            self.act(sq[:, k, 0:N], x[:, k, 0:N], AF.Square, [("x", k)], self.pg(4 + k // 2))
        bank = self.ring_ps()
        self.mm_group([(self.ps[bank][:, 0:N], self.onesb[:, :], sq[:, k, 0:N], k == 0, k == 7) for k in range(8)],
                      self.pg(4, 4) + ["onesb"], self.psk(bank))
        lnv = self.av(8, 1)[:, 0:N]
        rstd = self.av(9, 1)[:, 0:N]
        self.act(lnv, self.ps[bank][:, 0:N], AF.Ln, self.psk(bank) + ["epsc"], self.pg(8), bias=self.epsc[:, 0:1], scale=1.0 / D)
        self.act(rstd, lnv, AF.Exp, self.pg(8), self.pg(9), scale=-0.5)
        for k in range(8):
            hn = self.av(10 + (k % 2), 1)[:, 0:N]
            if grp == "p":
                self.stt(hn, x[:, k, 0:N], self.mod[:, l, mul_c0 + k, 0:1], rstd, ALU.mult, ALU.mult,
                         [("x", k), ("mod", l)] + self.pg(9), self.pg(10 + (k % 2)))
                self.act(h[:, k, 0:N], hn, AF.Identity, self.pg(10 + (k % 2)) + [("mod", l)], [("h", k)],
                         bias=self.mod[:, l, add_c0 + k, 0:1])
            else:
                v3 = lambda a: a.rearrange("p (s t) -> p s t", t=TS)
                self.tt(hn, x[:, k, 0:N], rstd, ALU.mult, [("x", k)] + self.pg(9), self.pg(10 + (k % 2)))
                self.tt(v3(hn), v3(hn), bcast(self.mod[:, l, mul_c0 + k, 1:1 + NSS], TS), ALU.mult,
                        self.pg(10 + (k % 2)) + [("mod", l)], self.pg(10 + (k % 2)))
                self.tt(v3(h[:, k, 0:N]), v3(hn), bcast(self.mod[:, l, add_c0 + k, 1:1 + NSS], TS), ALU.add,
                        self.pg(10 + (k % 2)) + [("mod", l)], [("h", k)])

    def gated_residual(self, N, l, j, bank, gate_c0, grp):
        x = self.x
        if grp == "p":
            self.stt(x[:, j, 0:N], self.ps[bank][:, 0:N], self.mod[:, l, gate_c0 + j, 0:1], x[:, j, 0:N], ALU.mult, ALU.add,
                     self.psk(bank) + [("x", j), ("mod", l)], [("x", j)])
        else:
            tmp = self.av(22, 1)[:, (j % 2) * 256:(j % 2) * 256 + N]
            v3 = lambda a: a.rearrange("p (s t) -> p s t", t=TS)
            self.tt(v3(tmp), v3(self.ps[bank][:, 0:N]), bcast(self.mod[:, l, gate_c0 + j, 1:1 + NSS], TS), ALU.mult,
                    self.psk(bank) + [("mod", l)], self.pg(22))
            self.tt(x[:, j, 0:N], x[:, j, 0:N], tmp, ALU.add, [("x", j)] + self.pg(22), [("x", j)])

    def hkeys(self):
        return [("h", k) for k in range(8)]

    def mlp(self, N, l, grp):
        h = self.h
        self.rmsnorm_mod(N, l, 32, 24, None, None, grp)
        hid = self.av(4, 16, BF16).rearrange("p (k n) -> p k n", k=32)
        for j4 in range(4):
            slot = self.w_acquire()
            wv = self.wring[slot][:, :].rearrange("p (k n) -> p k n", k=8)
            for jj in range(8):
                j = j4 * 8 + jj
                bank = self.ring_ps(8)
                self.mm_group([(self.ps[bank][:, 0:N], wv[:, k, jj * 128:(jj + 1) * 128], h[:, k, 0:N], k == 0, k == 7)
                               for k in range(8)], [("w", slot)] + self.hkeys(), self.psk(bank))
                rl = self.av(20 + (j % 2), 1)[:, 0:N]
                self.act(rl, self.ps[bank][:, 0:N], AF.Relu, self.psk(bank), self.pg(20 + (j % 2)))
                self.tt(hid[:, j, 0:N], rl, rl, ALU.mult, self.pg(20 + (j % 2)), [("hid", j)] + self.pg(4 + j // 2))
            self.w_release()
        for j4 in range(4):
            slot = self.w_acquire()
            wv = self.wring[slot][:, :].rearrange("p (k n) -> p k n", k=32)
            for jj in range(2):
                j = j4 * 2 + jj
                bank = self.ring_ps(8)
                self.mm_group([(self.ps[bank][:, 0:N], wv[:, k, jj * 128:(jj + 1) * 128], hid[:, k, 0:N], k == 0, k == 31)
                               for k in range(32)], [("w", slot)] + self.pg(4, 16), self.psk(bank))
                self.gated_residual(N, l, j, bank, 40, grp)
            self.w_release()

    def load_x(self, src_rows, nblk, rows_per_blk):
        for b in range(nblk):
            stg = self.av(4 + 2 * (b % 2), 2)
            self.dma("sp", stg[0:rows_per_blk, :], src_rows[b * rows_per_blk:(b + 1) * rows_per_blk, :], f"xin{b % 2}",
                     [], self.pg(4 + 2 * (b % 2), 2))
            for k4 in range(2):
                bank = self.ring_ps(8)
                for kk in range(4):
                    k = k4 * 4 + kk
                    self.transpose_f32(self.ps[bank][:, kk * 128:kk * 128 + rows_per_blk], stg[0:rows_per_blk, k * 128:(k + 1) * 128],
                                       rows_per_blk, self.pg(4 + 2 * (b % 2), 2), self.psk(bank))
                dst = self.x[:, k4 * 4:(k4 + 1) * 4, b * rows_per_blk:(b + 1) * rows_per_blk]
                src = self.ps[bank][:, :].rearrange("p (k n) -> p k n", k=4)[:, :, 0:rows_per_blk]
                self.copy(dst, src, self.psk(bank), [("x", k4 * 4 + i) for i in range(4)], eng=("act" if k4 else "dve"))

    def final_store(self, N, dst_rows, nblk, rows_per_blk, grp):
        x = self.x
        sq = self.av(4, 4, BF16).rearrange("p (k n) -> p k n", k=8)
        for k in range(8):
            self.act(sq[:, k, 0:N], x[:, k, 0:N], AF.Square, [("x", k)], self.pg(4 + k // 2))
        bank = self.ring_ps(8)
        self.mm_group([(self.ps[bank][:, 0:N], self.onesb[:, :], sq[:, k, 0:N], k == 0, k == 7) for k in range(8)],
                      self.pg(4, 4) + ["onesb"], self.psk(bank))
        lnv = self.av(8, 1)[:, 0:N]
        rstd = self.av(9, 1)[:, 0:N]
        self.act(lnv, self.ps[bank][:, 0:N], AF.Ln, self.psk(bank) + ["epsc"], self.pg(8), bias=self.epsc[:, 0:1], scale=1.0 / D)
        self.act(rstd, lnv, AF.Exp, self.pg(8), self.pg(9), scale=-0.5)
        yt = self.av(10, 8).rearrange("p (k n) -> p k n", k=8)
        for k in range(8):
            self.stt(yt[:, k, 0:N], x[:, k, 0:N], self.vecT[:, V_FINAL + k:V_FINAL + k + 1], rstd, ALU.mult, ALU.mult,
                     [("x", k), "vecT"] + self.pg(9), self.pg(10 + k))
        for b in range(nblk):
            stg = self.av(18 + 2 * (b % 2), 2)
            for k4 in range(2):
                bank = self.ring_ps(8)
                for kk in range(4):
                    k = k4 * 4 + kk
                    self.P.add("pe", lambda e, bank=bank, kk=kk, k=k, b=b: e.matmul(
                        self.ps[bank][0:rows_per_blk, kk * 128:(kk + 1) * 128], lhsT=yt[:, k, b * rows_per_blk:(b + 1) * rows_per_blk],
                        rhs=self.ident[:, :], start=True, stop=True), self.pg(10 + k) + ["ident"], self.psk(bank))
                self.copy(stg[0:rows_per_blk, k4 * 512:(k4 + 1) * 512], self.ps[bank][0:rows_per_blk, :], self.psk(bank),
                          self.pg(18 + 2 * (b % 2), 2), eng=("act" if k4 else "dve"))
            self.dma("sp", dst_rows[b * rows_per_blk:(b + 1) * rows_per_blk, :], stg[0:rows_per_blk, :], f"outy{b % 2}",
                     self.pg(18 + 2 * (b % 2), 2), [])

    def fm_proj(self, slot, col0, N, nring=4):
        wv = self.wring[slot][:, :].rearrange("p (k n) -> p k n", k=8)
        bank = self.ring_ps(nring)
        self.mm_group([(self.ps[bank][:, 0:N], wv[:, k, col0:col0 + 128], self.h[:, k, 0:N], k == 0, k == 7) for k in range(8)],
                      [("w", slot)] + self.hkeys(), self.psk(bank))
        return bank

    def build_diagw(self, l):
        VR = VROWS_PER_LAYER
        for c in range(2):
            for k in range(31):
                col = l * VR + V_CW + k * 2 + c
                self.ts(self.diagw[:, c, k, :], self.identb[:, :], self.vecT[:, col:col + 1], ALU.mult, ["identb", "vecT"], ["diagw"])

    def conv_branch(self, N, l, s0, aext_view, aext_key, a32, ln_pages=True):
        VR = VROWS_PER_LAYER
        mix = self.av(0, 4, BF16).rearrange("p (k n) -> p k n", k=8)
        sg = self.av(6, 1)[:, 0:N]
        for c in range(2):
            bv = self.fm_proj(s0, c * 128, N)
            bg = self.fm_proj(s0, 256 + c * 128, N)
            self.act(sg, self.ps[bg][:, 0:N], AF.Sigmoid, self.psk(bg), self.pg(6))
            self.tt(a32[c], self.ps[bv][:, 0:N], sg, ALU.mult, self.psk(bv) + self.pg(6), self.pg(4 + c))
            self.copy(aext_view(c, None), a32[c] if N == NT else a32[c].rearrange("p (s t) -> p s t", t=TS), self.pg(4 + c), [aext_key], eng="act")
        import os
        CUT = int(os.environ.get("CUT", "9"))
        if CUT <= 1:
            for c in range(2):
                self.P.add("dve", lambda e, c=c: e.memset(mix[:, c, :], 0.0), [], self.pg(0))
            return
        self.build_diagw(l)
        ac = [self.av(7 + c, 1)[:, 0:N] for c in range(2)]
        acsq = [self.av(9 + c, 1)[:, 0:N] for c in range(2)]
        for c in range(2):
            bank = self.ring_ps(4)
            self.mm_group([(self.ps[bank][:, 0:N], self.diagw[:, c, k, :], aext_view(c, k), k == 0, k == 30) for k in range(31)],
                          ["diagw", aext_key], self.psk(bank))
            cb = self.vecT[:, l * VR + V_CB + c:l * VR + V_CB + c + 1]
            self.act(ac[c], self.ps[bank][:, 0:N], AF.Identity, self.psk(bank) + ["vecT"], self.pg(7 + c), bias=cb)
            self.act(acsq[c], self.ps[bank][:, 0:N], AF.Square, self.psk(bank) + ["vecT"], self.pg(9 + c), bias=cb)
        if CUT <= 2:
            for c in range(2):
                self.P.add("dve", lambda e, c=c: e.memset(mix[:, c, :], 0.0), [], self.pg(0))
            return
        bm = self.ring_ps(4)
        self.mm_group([(self.ps[bm][:, 0:N], self.onesLN[:, :], ac[c], c == 0, c == 1) for c in range(2)], ["onesLN"] + self.pg(7, 2), self.psk(bm))
        bq = self.ring_ps(4)
        self.mm_group([(self.ps[bq][:, 0:N], self.onesLN[:, :], acsq[c], c == 0, c == 1) for c in range(2)], ["onesLN"] + self.pg(9, 2), self.psk(bq))
        mean = self.av(11, 1)[:, 0:N]
        var = self.av(12, 1)[:, 0:N]
        rstdc = self.av(13, 1)[:, 0:N]
        self.copy(mean, self.ps[bm][:, 0:N], self.psk(bm), self.pg(11), eng="act")
        self.tt(var, mean, mean, ALU.mult, self.pg(11), self.pg(12))
        self.tt(var, self.ps[bq][:, 0:N], var, ALU.subtract, self.psk(bq) + self.pg(12), self.pg(12))
        self.act(rstdc, var, AF.Ln, self.pg(12) + ["epsc"], self.pg(13), bias=self.epsc[:, 0:1])
        self.act(rstdc, rstdc, AF.Exp, self.pg(13), self.pg(13), scale=-0.5)
        if CUT <= 3:
            for c in range(2):
                self.P.add("dve", lambda e, c=c: e.memset(mix[:, c, :], 0.0), [], self.pg(0))
            return
        for c in range(2):
            xc = self.av(14, 1)[:, 0:N]
            self.tt(xc, ac[c], mean, ALU.subtract, self.pg(7 + c) + self.pg(11), self.pg(14))
            self.stt(xc, xc, self.vecT[:, l * VR + V_LNG + c:l * VR + V_LNG + c + 1], rstdc, ALU.mult, ALU.mult,
                     self.pg(14) + self.pg(13) + ["vecT"], self.pg(14))
            self.act(mix[:, c, 0:N], xc, AF.Silu, self.pg(14) + ["vecT"], self.pg(0),
                     bias=self.vecT[:, l * VR + V_LNB + c:l * VR + V_LNB + c + 1])

    def mixer_prompt(self, t, l, last):
        import os
        N = NT
        VR = VROWS_PER_LAYER
        P = self.P
        h = self.h
        mix = self.av(0, 4, BF16).rearrange("p (k n) -> p k n", k=8)
        self.rmsnorm_mod(N, l, 8, 0, None, None, "p")
        s0, s1, s2 = self.w_acquire(), self.w_acquire(), self.w_acquire()
        wv1 = self.wring[s1][:, :].rearrange("p (k n) -> p k n", k=8)
        wv2 = self.wring[s2][:, :].rearrange("p (k n) -> p k n", k=8)
        a32 = [self.av(4 + c, 1)[:, 0:N] for c in range(2)]
        aext_key = f"a_ext{l}"

        def aext_view(c, k):
            if k is None:
                return self.a_ext[:, l, c, 30:30 + N]
            return self.a_ext[:, l, c, k:k + N]
        self.conv_branch(N, l, s0, aext_view, aext_key, a32)
        if last and not os.environ.get("NOLASTC"):
            bank = self.ring_ps(4)
            for c in range(2):
                self.transpose_f32(self.ps[bank][:, c * 128:(c + 1) * 128], a32[c][:, N - 128:N], 128, self.pg(4 + c), self.psk(bank))
            stg = self.ostg[:, l, 0, :]
            self.copy(stg, self.ps[bank][:, 0:256], self.psk(bank), [("ostg", l, 0)])
            if "a" in os.environ.get("LASTSEL", "abc"):
                self.dma("sp", self.o["ncp"][l], stg[98:128, :], "outm", [("ostg", l, 0)], [])
        self.copy(self.a_ext[:, l, :, 0:30], self.a_ext[:, l, :, N:N + 30], [aext_key], [aext_key])
        if self.stage < 4:
            for k in range(2, 8):
                P.add("dve", lambda e, k=k: e.memset(mix[:, k, :], 0.0), [], self.pg(k // 2))
        if self.stage >= 4:
            qm = self.av(4, 2, BF16).rearrange("p (k n) -> p k n", k=4)
            kkey, vkey = f"kTa{l}", f"vatm{l}"
            P.add("dve", lambda e: e.memset(qm[:, :, :], 0.0), [], self.pg(4, 2))
            for p in range(2):
                bk = self.fm_proj(s2, 512 + p * 128, N)
                for c in range(2):
                    self.act(qm[c * 64:(c + 1) * 64, p * 2 + c, :], self.ps[bk][c * 64:(c + 1) * 64, 0:N], AF.Identity, self.psk(bk), self.pg(4, 2), scale=0.125)
            bk = self.fm_proj(s2, 768, N)
            self.copy(self.kTa[:, l, 128:128 + N], self.ps[bk][:, 0:N], self.psk(bk), [kkey], eng="act")
            for b in range(4):
                bank = self.ring_ps(4)
                self.mm_group([(self.ps[bank][:, 0:256], h[:, k, b * 128:(b + 1) * 128], wv2[:, k, 768:1024], k == 0, k == 7) for k in range(8)],
                              [("w", s2)] + self.hkeys(), self.psk(bank))
                for c in range(2):
                    self.copy(self.vatm[:, l, 1 + b, c, c * 64:(c + 1) * 64], self.ps[bank][:, 128 + c * 64:128 + (c + 1) * 64], self.psk(bank), [vkey])
                if last and b == 3 and not os.environ.get("NOLASTA"):
                    stg = self.ostg[:, l, 1, :]
                    self.copy(stg, self.ps[bank][:, 0:256], self.psk(bank), [("ostg", l, 1)], eng="act")
                    if "c" in os.environ.get("LASTSEL", "abc"):
                        self.dma("sp", self.o["nkp"][l], stg[:, 0:128], "outm", [("ostg", l, 1)], [])
                        self.dma("sp", self.o["nvp"][l], stg[:, 128:256], "outm", [("ostg", l, 1)], [])
            E = [self.av(6 + w, 1) for w in range(2)]
            PT = self.av(8, 1, BF16).rearrange("p (w n) -> p w n", w=2)
            for b in range(4):
                first = (t == 0 and b == 0)
                for w in range(2):
                    if w == 0 and first:
                        continue
                    bank = self.ring_ps(4)
                    grp = []
                    for hs in range(4):
                        grp.append((self.ps[bank][:, hs * 128:(hs + 1) * 128], self.kTa[:, l, (b + w) * 128:(b + w + 1) * 128],
                                    qm[:, hs, b * 128:(b + 1) * 128], True, True))
                    import os
                    SUB = int(os.environ.get("SUB", "9"))
                    self.mm_group(grp, [kkey] + self.pg(4, 2), self.psk(bank))
                    if SUB >= 2:
                        self.act(E[w][:, :], self.ps[bank][:, :], AF.Exp, self.psk(bank), self.pg(6 + w))
                    if SUB >= 3:
                        self.tt(PT[:, w, :], E[w][:, :], self.EBp[:, w, :, :].rearrange("p h q -> p (h q)"), ALU.mult, self.pg(6 + w) + ["EBp"], self.pg(8))
                if SUB < 9:
                    continue
                ws = (1,) if first else (0, 1)
                for p in range(2):
                    seq = [(c, w) for c in range(2) for w in ws]
                    grp = []
                    for i, (c, w) in enumerate(seq):
                        grp.append((self.ps[4 + p][:, b * 128:(b + 1) * 128], self.vatm[:, l, b + w, c, :],
                                    PT[:, w, (p * 2 + c) * 128:(p * 2 + c + 1) * 128], i == 0, i == len(seq) - 1))
                    for i, (c, w) in enumerate(seq):
                        grp.append((self.ps[6 + p][:, b * 128:(b + 1) * 128], self.onespad[:, c, :],
                                    PT[:, w, (p * 2 + c) * 128:(p * 2 + c + 1) * 128], i == 0, i == len(seq) - 1))
                    self.mm_group(grp, [vkey, "onespad"] + self.pg(8), [("ps", 4 + p), ("ps", 6 + p)])
            den = self.av(9, 1)
            ACUT = 9
            if ACUT <= 3:
                for k in range(6, 8):
                    P.add("dve", lambda e, k=k: e.memset(mix[:, k, :], 0.0), [], self.pg(k // 2))
            import os
            if int(os.environ.get("SUB", "9")) < 9:
                for k in range(6, 8):
                    P.add("dve", lambda e, k=k: e.memset(mix[:, k, :], 0.0), [], self.pg(k // 2))
            for p in range(2 if int(os.environ.get("SUB", "9")) >= 9 else 0):
                self.ts(den[:, :], self.ps[6 + p][:, :], self.esink[:, l * 2 + p:l * 2 + p + 1], ALU.add, self.psk(6 + p) + ["esink"], self.pg(9))
                P.add("dve", lambda e: e.reciprocal(out=den[:, :], in_=den[:, :]), self.pg(9), self.pg(9))
                self.tt(mix[:, 6 + p, :], self.ps[4 + p][:, :], den[:, :], ALU.mult, self.psk(4 + p) + self.pg(9), self.pg(3))
            self.copy(self.kTa[:, l, 0:128], self.kTa[:, l, N:N + 128], [kkey], [kkey])
            self.copy(self.vatm[:, l, 0, :, :], self.vatm[:, l, 4, :, :], [vkey], [vkey])
        if self.stage == 4:
            for k in range(2, 6):
                P.add("dve", lambda e, k=k: e.memset(mix[:, k, :], 0.0), [], self.pg(k // 2))
        if self.stage >= 5 and HGRN_ON:
            self.hgrn_prompt(l, s0, s1, s2, last)
        elif self.stage >= 5:
            for k in range(2, 6):
                P.add("dve", lambda e, k=k: e.memset(mix[:, k, :], 0.0), [], self.pg(k // 2))
        for _ in range(3):
            self.w_release()
        s3 = self.w_acquire()
        wv3 = self.wring[s3][:, :].rearrange("p (k n) -> p k n", k=8)
        for j in range(8):
            bank = self.ring_ps(8)
            self.mm_group([(self.ps[bank][:, 0:N], wv3[:, k, j * 128:(j + 1) * 128], mix[:, k, 0:N], k == 0, k == 7) for k in range(8)],
                          [("w", s3)] + self.pg(0, 4), self.psk(bank))
            self.gated_residual(N, l, j, bank, 16, "p")
        self.w_release()

    def hgrn_prompt(self, l, s0, s1, s2, last):
        N = NT
        VR = VROWS_PER_LAYER
        P = self.P
        h = self.h
        mix = self.av(0, 4, BF16).rearrange("p (k n) -> p k n", k=8)
        wv1 = self.wring[s1][:, :].rearrange("p (k n) -> p k n", k=8)
        vtm = self.av(14, 2, BF16).rearrange("p (b n) -> p b n", b=4)
        for b in range(4):
            bank = self.ring_ps(4)
            self.mm_group([(self.ps[bank][:, :], h[:, k, b * 128:(b + 1) * 128], wv1[:, k, 512:1024], k == 0, k == 7) for k in range(8)],
                          [("w", s1)] + self.hkeys(), self.psk(bank))
            self.copy(vtm[:, b, :], self.ps[bank][:, :], self.psk(bank), self.pg(14 + b // 2), eng="act")
        skey = f"S{l}"
        sig, lf, kraw, brel, d1, eq, ek, ktf = [self.av(4 + i, 1) for i in range(8)]
        qt = self.av(12, 1, BF16)[:, 0:512]
        ktb = self.av(12, 1, BF16)[:, 512:1024]
        sgate = self.av(13, 1)
        osq = self.av(16, 1, BF16)[:, 0:512]
        ktm = self.av(16, 1, BF16)[:, 512:1024].rearrange("p (b n) -> p b n", b=4)
        rr = self.av(17, 1)
        t1 = self.av(18, 1)
        Sbf = self.av(19, 2, BF16).rearrange("p (c n) -> p c n", c=16)
        chs = self.chs
        for hh in range(4):
            bq = self.fm_proj(s0, 512 + hh * 128, N)
            bf = self.fm_proj(s1, hh * 128, N)
            bg = self.fm_proj(s2, hh * 128, N)
            lbc = self.lbv[:, l, 0, hh:hh + 1]
            omlc = self.lbv[:, l, 1, hh:hh + 1]
            nomlc = self.lbv[:, l, 2, hh:hh + 1]
            import os
            HS = int(os.environ.get("HS", "9"))
            if HS >= 2:
                self.act(sig[:, :], self.ps[bf][:, :], AF.Sigmoid, self.psk(bf), self.pg(4))
                self.ts(lf[:, :], sig[:, :], omlc, ALU.mult, self.pg(4) + ["lbv"], self.pg(5), lbc, ALU.add)
                self.act(lf[:, :], lf[:, :], AF.Ln, self.pg(5), self.pg(5))
                self.ts(kraw[:, :], sig[:, :], nomlc, ALU.mult, self.pg(4) + ["lbv"], self.pg(6), omlc, ALU.add)
            if HS >= 3:
                P.add("dve", lambda e: e.tensor_tensor_scan(out=brel[:, :], data0=self.rstp[:, :], data1=lf[:, :], initial=0.0,
                                                            op0=ALU.mult, op1=ALU.add), self.pg(5) + ["rstp"], self.pg(7))
            br3 = brel[:, :].rearrange("p (c n) -> p c n", c=16)
            mid = br3[:, :, 15]
            bL = br3[:, :, 31]
            if HS >= 4:
                self.tt(d1[:, :].rearrange("p (c n) -> p c n", c=16), br3, bcast(mid, 32), ALU.subtract, self.pg(7), self.pg(8))
                self.act(eq[:, :], d1[:, :], AF.Exp, self.pg(8), self.pg(9))
                self.act(ek[:, :], d1[:, :], AF.Exp, self.pg(8), self.pg(10), scale=-1.0)
                self.tt(qt, self.ps[bq][:, :], eq[:, :], ALU.mult, self.psk(bq) + self.pg(9), self.pg(12))
                self.tt(ktf[:, :], kraw[:, :], ek[:, :], ALU.mult, self.pg(6) + self.pg(10), self.pg(11))
                self.copy(ktb, ktf[:, :], self.pg(11), self.pg(12), eng="act")
            if HS >= 5:
                self.act(chs[:, 0, :], mid, AF.Exp, self.pg(7), ["chs"])
                self.act(chs[:, 1, :], bL, AF.Exp, self.pg(7), ["chs"])
                self.tt(chs[:, 3, :], bL, mid, ALU.subtract, self.pg(7), ["chs"])
                self.act(chs[:, 2, :], chs[:, 3, :], AF.Exp, ["chs"], ["chs"])
            if HS >= 6:
                self.act(sgate[:, :], self.ps[bg][:, :], AF.Silu, self.psk(bg), self.pg(13))

            import os
            HC = int(os.environ.get("HC", "9"))
            if HC <= 1:
                P.add("dve", lambda e, hh=hh: e.memset(mix[:, 2 + hh, :], 0.0), [], self.pg(1 + hh // 2))
                continue
            bankT = self.ring_ps(4)
            for b in range(4):
                self.transpose_f32(self.ps[bankT][:, b * 128:(b + 1) * 128], ktf[:, b * 128:(b + 1) * 128], 128, self.pg(11), self.psk(bankT))
            kx = self.av(21, 2, BF16)
            pT = self.ps[bankT][:, :]
            pa = [list(x) for x in pT.ap]
            in0 = bass.AP(tensor=pT.tensor, offset=pT.offset, ap=[pa[0], [128, 4], [0, 4], [1, 128]])
            cm = self.cmask[:, :]
            ca = [list(x) for x in cm.ap]
            in1 = bass.AP(tensor=cm.tensor, offset=cm.offset, ap=[ca[0], [0, 4], [1, 4], [0, 128]])
            kx4 = kx.rearrange("p (b c n) -> p b c n", b=4, c=4)
            self.tt(kx4, in0, in1, ALU.mult, self.psk(bankT) + ["cmask"], self.pg(21, 2))
            bU = [self.ring_ps(4) for _ in range(4)]
            for q4 in range(4):
                grp = []
                for cc in range(4):
                    grp.append((self.ps[bU[q4]][:, cc * 128:(cc + 1) * 128], kx4[:, q4, cc, :],
                                vtm[:, q4, hh * 128:(hh + 1) * 128], True, True))
                self.mm_group(grp, self.pg(21, 2) + self.pg(14, 2), self.psk(bU[q4]))
            if HC <= 2:
                P.add("dve", lambda e, hh=hh: e.memset(mix[:, 2 + hh, :], 0.0), [], self.pg(1 + hh // 2))
                continue
            Sh = self.S[:, l, hh, :]
            for c in range(16):
                self.ts(Sbf[:, c, :], Sh, chs[:, 0, c:c + 1], ALU.mult, [skey, "chs"], self.pg(19 + c // 8))
                wt = self.wt[:, c % 2, :]
                self.ts(wt, self.ps[bU[c // 4]][:, (c % 4) * 128:(c % 4 + 1) * 128], chs[:, 2, c:c + 1], ALU.mult,
                        self.psk(bU[c // 4]) + ["chs"], [("wt", c % 2)])
                self.stt(Sh, Sh, chs[:, 1, c:c + 1], wt, ALU.mult, ALU.add, [skey, "chs", ("wt", c % 2)], [skey])
            bankA = self.ring_ps(4)
            self.mm_group([(self.ps[bankA][:, b * 128:(b + 1) * 128], ktb[:, b * 128:(b + 1) * 128], qt[:, b * 128:(b + 1) * 128], True, True)
                           for b in range(4)], self.pg(12), self.psk(bankA))
            self.tt(self.am[:, :, :], self.ps[bankA][:, :].rearrange("p (b n) -> p b n", b=4), bcast_mid(self.mAp[:, :], 4), ALU.mult,
                    self.psk(bankA) + ["mAp"], ["am"])
            if HC <= 3:
                P.add("dve", lambda e, hh=hh: e.memset(mix[:, 2 + hh, :], 0.0), [], self.pg(1 + hh // 2))
                continue
            ob = 4 + hh
            grp = []
            for b in range(4):
                grp.append((self.ps[ob][:, b * 128:(b + 1) * 128], vtm[:, b, hh * 128:(hh + 1) * 128], self.am[:, b, :], True, False))
                for c in range(4 * b, 4 * b + 4):
                    grp.append((self.ps[ob][:, c * 32:(c + 1) * 32], Sbf[:, c, :], qt[:, c * 32:(c + 1) * 32], False, c == 4 * b + 3))
            self.mm_group(grp, self.pg(14, 2) + ["am"] + self.pg(19, 2) + self.pg(12), self.psk(ob))
            self.act(osq, self.ps[ob][:, :], AF.Square, self.psk(ob), self.pg(16))
            bs = self.ring_ps(4)
            self.mm_group([(self.ps[bs][:, :], self.onesb[:, :], osq, True, True)], ["onesb"] + self.pg(16), self.psk(bs))
            self.act(rr[:, :], self.ps[bs][:, :], AF.Ln, self.psk(bs) + ["epsc"], self.pg(17), bias=self.epsc[:, 0:1], scale=1.0 / 128.0)
            self.act(rr[:, :], rr[:, :], AF.Exp, self.pg(17), self.pg(17), scale=-0.5)
            self.tt(t1[:, :], self.ps[ob][:, :], rr[:, :], ALU.mult, self.psk(ob) + self.pg(17), self.pg(18))
            self.stt(mix[:, 2 + hh, :], t1[:, :], self.vecT[:, l * VR + V_HG:l * VR + V_HG + 1], sgate[:, :], ALU.mult, ALU.mult,
                     self.pg(18) + self.pg(13) + ["vecT"], self.pg(1 + hh // 2))
        import os
        if last and "b" in os.environ.get("LASTSEL", "abc"):
            self.dma("sp", self.o["nhp"][l].rearrange("h d v -> d h v"), self.S[:, l, :, :], "outm", [skey], [])

    def tile_sample(self):
        N = NSS * TS
        self.load_x(self.d["xs"], 1, N)
        for l in range(DEPTH):
            self.mixer_sample(l)
            self.mlp(N, l, "s")
        self.final_store(N, self.o["ys"], 1, N, "s")

    def mixer_sample(self, l):
        N = NSS * TS
        VR = VROWS_PER_LAYER
        P, d, o = self.P, self.d, self.o
        h = self.h
        mix = self.av(0, 4, BF16).rearrange("p (k n) -> p k n", k=8)
        self.rmsnorm_mod(N, l, 8, 0, None, None, "s")
        s0, s1, s2 = self.w_acquire(), self.w_acquire(), self.w_acquire()
        wv1 = self.wring[s1][:, :].rearrange("p (k n) -> p k n", k=8)
        wv2 = self.wring[s2][:, :].rearrange("p (k n) -> p k n", k=8)
        cc3 = d["cc"][l].rearrange("(s r) c -> s r c", r=30)
        self.dma("sp", o["ncs"][l][:, 0:26, :], cc3[:, 4:30, :], "outm", [], [])
        for g in range(4):
            stg = self.av(15, 1)[0:120, 0:256]
            self.dma("sp", stg, d["cc"][l][g * 120:(g + 1) * 120, :], "cin", [], self.pg(15))
            for c in range(2):
                bank = self.ring_ps(4)
                self.transpose_f32(self.ps[bank][:, 0:120], stg[:, c * 128:(c + 1) * 128], 120, self.pg(15), self.psk(bank))
                self.copy(self.a_exts[:, c, 4 * g:4 * g + 4, 0:30], self.ps[bank][:, 0:120].rearrange("p (s r) -> p s r", r=30),
                          self.psk(bank), ["a_exts"])
        a32 = [self.av(4 + c, 1)[:, 0:N] for c in range(2)]

        def aext_view(c, k):
            if k is None:
                return self.a_exts[:, c, :, 30:30 + TS]
            return self.a_exts[:, c, :, k:k + TS]
        self.conv_branch(N, l, s0, aext_view, "a_exts", a32)
        bank = self.ring_ps(4)
        for c in range(2):
            self.P.add("pe", lambda e, c=c, bank=bank: e.matmul(self.ps[bank][0:N, c * 128:(c + 1) * 128], lhsT=a32[c], rhs=self.ident[:, :],
                                                                start=True, stop=True), self.pg(4 + c) + ["ident"], self.psk(bank))
        self.copy(self.ostgs[:, l, 0, :], self.ps[bank][0:N, 0:256], self.psk(bank), [("ostgs", l, 0)])
        for sq in range(NSS):
            self.dma("sp", o["ncs"][l][sq, 26:30, :], self.ostgs[TS * sq:TS * sq + TS, l, 0, :], "outm", [("ostgs", l, 0)], [])
        qm = self.av(4, 1, BF16)[:, 0:4 * N].rearrange("p (k n) -> p k n", k=4)
        kTn = self.av(5, 1, BF16)[:, 0:N]
        P.add("dve", lambda e: e.memset(qm, 0.0), [], self.pg(4))
        for p in range(2):
            bk = self.fm_proj(s2, 512 + p * 128, N)
            for c in range(2):
                self.act(qm[c * 64:(c + 1) * 64, p * 2 + c, :], self.ps[bk][c * 64:(c + 1) * 64, 0:N], AF.Identity, self.psk(bk), self.pg(4), scale=0.125)
        bk = self.fm_proj(s2, 768, N)
        self.copy(kTn, self.ps[bk][:, 0:N], self.psk(bk), self.pg(5), eng="act")
        bank = self.ring_ps(4)
        self.mm_group([(self.ps[bank][0:N, 0:256], h[:, k, 0:N], wv2[:, k, 768:1024], k == 0, k == 7) for k in range(8)],
                      [("w", s2)] + self.hkeys(), self.psk(bank))
        vnp = self.av(19, 1, BF16)[0:N, 0:256].rearrange("p (c n) -> p c n", c=2)
        P.add("dve", lambda e: e.memset(vnp, 0.0), [], self.pg(19))
        for c in range(2):
            self.copy(vnp[:, c, c * 64:(c + 1) * 64], self.ps[bank][0:N, 128 + c * 64:128 + (c + 1) * 64], self.psk(bank), self.pg(19))
        self.copy(self.ostgs[:, l, 1, :], self.ps[bank][0:N, 0:256], self.psk(bank), [("ostgs", l, 1)])
        self.dma("sp", o["nks"][l][:, 0:124, :], d["ck"][l][:, 4:128, :], "outm", [], [])
        self.dma("sp", o["nvs"][l][:, 0:124, :], d["cv"][l][:, 4:128, :], "outm", [], [])
        for sq in range(NSS):
            self.dma("sp", o["nks"][l][sq, 124:128, :], self.ostgs[TS * sq:TS * sq + TS, l, 1, 0:128], "outm", [("ostgs", l, 1)], [])
            self.dma("sp", o["nvs"][l][sq, 124:128, :], self.ostgs[TS * sq:TS * sq + TS, l, 1, 128:256], "outm", [("ostgs", l, 1)], [])
        c32 = self.av(6, 4).rearrange("p (s n) -> p s n", s=NSS)
        kcT = self.av(10, 2, BF16).rearrange("p (s n) -> p s n", s=NSS)
        vcp = self.av(12, 4, BF16).rearrange("p (s c n) -> p s c n", s=NSS, c=2)
        self.dma("sp", c32, d["ck"][l].rearrange("s k f -> k s f"), "cin", [], self.pg(6, 4))
        for g in range(4):
            bank = self.ring_ps(4)
            for i in range(4):
                self.transpose_f32(self.ps[bank][:, i * 128:(i + 1) * 128], c32[:, 4 * g + i, :], 128, self.pg(6, 4), self.psk(bank))
            self.copy(kcT[:, 4 * g:4 * g + 4, :], self.ps[bank][:, :].rearrange("p (s n) -> p s n", s=4), self.psk(bank), self.pg(10, 2),
                      eng=("act" if g % 2 else "dve"))
        self.dma("sp", c32, d["cv"][l].rearrange("s k f -> k s f"), "cin", [], self.pg(6, 4))
        P.add("dve", lambda e: e.memset(vcp, 0.0), [], self.pg(12, 4))
        for c in range(2):
            self.copy(vcp[:, :, c, c * 64:(c + 1) * 64], c32[:, :, c * 64:(c + 1) * 64], self.pg(6, 4), self.pg(12, 4), eng=("act" if c else "dve"))
        bank = self.ring_ps(4)
        grp = []
        for hs in range(4):
            for sq in range(NSS):
                grp.append((self.ps[bank][:, hs * N + TS * sq:hs * N + TS * sq + TS], kcT[:, sq, :], qm[:, hs, TS * sq:TS * sq + TS], True, True))
        self.mm_group(grp, self.pg(10, 2) + self.pg(4), self.psk(bank))
        E = self.av(16, 1)[:, 0:4 * N]
        PTc = self.av(17, 1, BF16)[:, 0:4 * N].rearrange("p (k n) -> p k n", k=4)
        self.act(E, self.ps[bank][:, 0:4 * N], AF.Exp, self.psk(bank), self.pg(16))
        eb = self.EBsc[:, :, :]
        ea = [list(x) for x in eb.ap]
        eb_b = bass.AP(tensor=eb.tensor, offset=eb.offset, ap=[ea[0], [TS, 4], [0, NSS], [1, TS]])
        self.tt(PTc.rearrange("p k (s t) -> p k s t", t=TS), E.rearrange("p (k s t) -> p k s t", k=4, t=TS), eb_b, ALU.mult,
                self.pg(16) + ["EBsc"], self.pg(17))
        bank2 = self.ring_ps(4)
        self.mm_group([(self.ps[bank2][0:N, hs * N:(hs + 1) * N], kTn, qm[:, hs, :], True, True) for hs in range(4)],
                      self.pg(5) + self.pg(4), self.psk(bank2))
        E2 = self.av(18, 1)[0:N, 0:4 * N]
        PTn = self.av(17, 1, BF16)[0:N, 512:512 + 4 * N].rearrange("p (k n) -> p k n", k=4)
        self.act(E2, self.ps[bank2][0:N, 0:4 * N], AF.Exp, self.psk(bank2), self.pg(18))
        self.tt(PTn, E2.rearrange("p (k n) -> p k n", k=4), self.EBsn[:, :, :], ALU.mult, self.pg(18) + ["EBsn"], self.pg(17))
        for p in range(2):
            grp = []
            for c in range(2):
                grp.append((self.ps[4 + p][:, 0:N], vnp[:, c, :], PTn[:, p * 2 + c, :], c == 0, False))
            for sq in range(NSS):
                for c in range(2):
                    grp.append((self.ps[4 + p][:, TS * sq:TS * sq + TS], vcp[:, sq, c, :], PTc[:, p * 2 + c, TS * sq:TS * sq + TS], False,
                                sq == NSS - 1 and c == 1))
            seq = [("c", 0), ("c", 1), ("n", 0), ("n", 1)]
            for i, (kind, c) in enumerate(seq):
                if kind == "c":
                    grp.append((self.ps[6 + p][:, 0:N], self.onespad[:, c, :], PTc[:, p * 2 + c, :], i == 0, False))
                else:
                    grp.append((self.ps[6 + p][:, 0:N], self.onespad[0:N, c, :], PTn[:, p * 2 + c, :], False, i == 3))
            self.mm_group(grp, self.pg(12, 4) + self.pg(17) + self.pg(19) + ["onespad"], [("ps", 4 + p), ("ps", 6 + p)])
        den = self.av(20, 1)[:, 0:N]
        for p in range(2):
            self.ts(den, self.ps[6 + p][:, 0:N], self.esink[:, l * 2 + p:l * 2 + p + 1], ALU.add, self.psk(6 + p) + ["esink"], self.pg(20))
            P.add("dve", lambda e: e.reciprocal(out=den, in_=den), self.pg(20), self.pg(20))
            self.tt(mix[:, 6 + p, 0:N], self.ps[4 + p][:, 0:N], den, ALU.mult, self.psk(4 + p) + self.pg(20), self.pg(3))
        vtm = self.av(14, 1, BF16)[0:N, 0:512]
        bank = self.ring_ps(4)
        self.mm_group([(self.ps[bank][0:N, :], h[:, k, 0:N], wv1[:, k, 512:1024], k == 0, k == 7) for k in range(8)],
                      [("w", s1)] + self.hkeys(), self.psk(bank))
        self.copy(vtm, self.ps[bank][0:N, :], self.psk(bank), self.pg(14), eng="act")
        sig, lf, kraw, brel, d1, eq, ek, ktf = [self.av(4 + i, 1)[:, 0:N] for i in range(8)]
        qt = self.av(12, 1, BF16)[:, 0:N]
        ktb = self.av(12, 1, BF16)[:, 512:512 + N]
        sgate = self.av(13, 1)[:, 0:N]
        osq = self.av(14, 1, BF16)[:, 512:512 + N]
        rr = self.av(8, 1)[:, 0:N]
        t1 = self.av(9, 1)[:, 0:N]
        t2 = self.av(10, 1)
        S0 = self.av(15, 4).rearrange("p (s n) -> p s n", s=NSS)
        Sbf = self.av(19, 2, BF16).rearrange("p (s n) -> p s n", s=NSS)
        kx = self.av(21, 2, BF16)[0:N, :].rearrange("p (s n) -> p s n", s=NSS)
        ams = self.am[0:N, 0, 0:N]
        chs = self.chs
        for hh in range(4):
            bq = self.fm_proj(s0, 512 + hh * 128, N)
            bf = self.fm_proj(s1, hh * 128, N)
            bg = self.fm_proj(s2, hh * 128, N)
            lbc = self.lbv[:, l, 0, hh:hh + 1]
            omlc = self.lbv[:, l, 1, hh:hh + 1]
            nomlc = self.lbv[:, l, 2, hh:hh + 1]
            self.dma("sp", S0, d["sh"][l][:, hh].rearrange("s d v -> d s v"), "cin", [], self.pg(15, 4))
            self.act(sig, self.ps[bf][:, 0:N], AF.Sigmoid, self.psk(bf), self.pg(4))
            self.ts(lf, sig, omlc, ALU.mult, self.pg(4) + ["lbv"], self.pg(5), lbc, ALU.add)
            self.act(lf, lf, AF.Ln, self.pg(5), self.pg(5))
            self.ts(kraw, sig, nomlc, ALU.mult, self.pg(4) + ["lbv"], self.pg(6), omlc, ALU.add)
            P.add("dve", lambda e: e.tensor_tensor_scan(out=brel, data0=self.rsts[:, :], data1=lf, initial=0.0,
                                                        op0=ALU.mult, op1=ALU.add), self.pg(5) + ["rsts"], self.pg(7))
            br3 = brel.rearrange("p (c n) -> p c n", n=TS)
            mid = br3[:, :, 1]
            bL = br3[:, :, 3]
            self.tt(d1.rearrange("p (c n) -> p c n", n=TS), br3, bcast(mid, TS), ALU.subtract, self.pg(7), self.pg(8))
            self.act(eq, d1, AF.Exp, self.pg(8), self.pg(9))
            self.act(ek, d1, AF.Exp, self.pg(8), self.pg(10), scale=-1.0)
            self.tt(qt, self.ps[bq][:, 0:N], eq, ALU.mult, self.psk(bq) + self.pg(9), self.pg(12))
            self.tt(ktf, kraw, ek, ALU.mult, self.pg(6) + self.pg(10), self.pg(11))
            self.copy(ktb, ktf, self.pg(11), self.pg(12), eng="act")
            self.act(chs[:, 0, :], mid, AF.Exp, self.pg(7), ["chs"])
            self.act(chs[:, 1, :], bL, AF.Exp, self.pg(7), ["chs"])
            self.tt(chs[:, 3, :], bL, mid, ALU.subtract, self.pg(7), ["chs"])
            self.act(chs[:, 2, :], chs[:, 3, :], AF.Exp, ["chs"], ["chs"])
            self.act(sgate, self.ps[bg][:, 0:N], AF.Silu, self.psk(bg), self.pg(13))
            bankT = self.ring_ps(4)
            self.P.add("pe", lambda e, bankT=bankT: e.matmul(self.ps[bankT][0:N, 0:128], lhsT=ktf, rhs=self.ident[:, :], start=True, stop=True),
                       self.pg(11) + ["ident"], self.psk(bankT))
            pT = self.ps[bankT][0:N, 0:128]
            pa = [list(x) for x in pT.ap]
            in0 = bass.AP(tensor=pT.tensor, offset=pT.offset, ap=[pa[0], [0, NSS], [1, 128]])
            self.tt(kx, in0, bcast(self.seqoh[:, :], 128), ALU.mult, self.psk(bankT) + ["seqoh"], self.pg(21, 2))
            bU = [self.ring_ps(4) for _ in range(4)]
            for q4 in range(4):
                self.mm_group([(self.ps[bU[q4]][:, i * 128:(i + 1) * 128], kx[:, 4 * q4 + i, :], vtm[:, hh * 128:(hh + 1) * 128], True, True)
                               for i in range(4)], self.pg(21, 2) + self.pg(14), self.psk(bU[q4]))
            self.tt(Sbf, S0, bcast(chs[:, 0, :], 128), ALU.mult, self.pg(15, 4) + ["chs"], self.pg(19, 2))
            self.tt(S0, S0, bcast(chs[:, 1, :], 128), ALU.mult, self.pg(15, 4) + ["chs"], self.pg(15, 4))
            for q4 in range(4):
                t23 = t2.rearrange("p (s n) -> p s n", s=4)
                self.tt(t23, self.ps[bU[q4]][:, :].rearrange("p (s n) -> p s n", s=4), bcast(chs[:, 2, 4 * q4:4 * q4 + 4], 128), ALU.mult,
                        self.psk(bU[q4]) + ["chs"], self.pg(10))
                self.tt(S0[:, 4 * q4:4 * q4 + 4, :], S0[:, 4 * q4:4 * q4 + 4, :], t23, ALU.add, self.pg(15, 4) + self.pg(10), self.pg(15, 4))
            self.dma("sp", o["nhs"][l][:, hh].rearrange("s d v -> d s v"), S0, "outh", self.pg(15, 4), [])
            bankA = self.ring_ps(4)
            self.mm_group([(self.ps[bankA][0:N, 0:N], ktb, qt, True, True)], self.pg(12), self.psk(bankA))
            self.tt(ams, self.ps[bankA][0:N, 0:N], self.mAs[:, :], ALU.mult, self.psk(bankA) + ["mAs"], ["am"])
            ob = 4 + hh
            grp = [(self.ps[ob][:, 0:N], vtm[:, hh * 128:(hh + 1) * 128], ams, True, False)]
            for sq in range(NSS):
                grp.append((self.ps[ob][:, TS * sq:TS * sq + TS], Sbf[:, sq, :], qt[:, TS * sq:TS * sq + TS], False, sq == NSS - 1))
            self.mm_group(grp, self.pg(14) + ["am"] + self.pg(19, 2) + self.pg(12), self.psk(ob))
            self.act(osq, self.ps[ob][:, 0:N], AF.Square, self.psk(ob), self.pg(14))
            bs = self.ring_ps(4)
            self.mm_group([(self.ps[bs][:, 0:N], self.onesb[:, :], osq, True, True)], ["onesb"] + self.pg(14), self.psk(bs))
            self.act(rr, self.ps[bs][:, 0:N], AF.Ln, self.psk(bs) + ["epsc"], self.pg(8), bias=self.epsc[:, 0:1], scale=1.0 / 128.0)
            self.act(rr, rr, AF.Exp, self.pg(8), self.pg(8), scale=-0.5)
            self.tt(t1, self.ps[ob][:, 0:N], rr, ALU.mult, self.psk(ob) + self.pg(8), self.pg(9))
            self.stt(mix[:, 2 + hh, 0:N], t1, self.vecT[:, l * VR + V_HG:l * VR + V_HG + 1], sgate, ALU.mult, ALU.mult,
                     self.pg(9) + self.pg(13) + ["vecT"], self.pg(1 + hh // 2))
        for _ in range(3):
            self.w_release()
        s3 = self.w_acquire()
        wv3 = self.wring[s3][:, :].rearrange("p (k n) -> p k n", k=8)
        for j in range(8):
            bank = self.ring_ps(8)
            self.mm_group([(self.ps[bank][:, 0:N], wv3[:, k, j * 128:(j + 1) * 128], mix[:, k, 0:N], k == 0, k == 7) for k in range(8)],
                          [("w", s3)] + self.pg(0, 4), self.psk(bank))
            self.gated_residual(N, l, j, bank, 16, "s")
        self.w_release()

    def tile_prompt(self, t, last):
        self.load_x(self.d["xp"][t * NT:(t + 1) * NT, :], 4, 128)
        for l in range(DEPTH):
            if self.stage >= 3:
                self.mixer_prompt(t, l, last)
            else:
                for _ in range(4):
                    self.w_acquire()
                    self.w_release()
            if self.stage >= 2:
                self.mlp(NT, l, "p")
            else:
                for _ in range(8):
                    self.w_acquire()
                    self.w_release()
        self.final_store(NT, self.o["yp"][t * NT:(t + 1) * NT, :], 4, 128, "p")


_CACHE = {}


def _get_nc(stage=60):
    if stage not in _CACHE:
        b = Builder(stage)
        _CACHE[stage] = b.build()
    return _CACHE[stage]


def _pack_vecs(inp):
    rows = []
    for l in range(DEPTH):
        rows.append(inp["b_ada"][l].reshape(48, 128))
        rows.append(inp["norm_mix_g"][l].reshape(8, 128))
        rows.append(inp["norm_mlp_g"][l].reshape(8, 128))
        rows.append(inp["conv_b"][l].reshape(2, 128))
        rows.append(inp["conv_ln_g"][l].reshape(2, 128))
        rows.append(inp["conv_ln_b"][l].reshape(2, 128))
        rows.append(inp["hgrn_lb"][l].reshape(4, 128))
        rows.append(inp["hgrn_norm_g"][l].reshape(1, 128))
        rows.append(inp["conv_w"][l].reshape(31 * 2, 128))
    rows.append(inp["final_g"].reshape(8, 128))
    v = np.ascontiguousarray(np.concatenate(rows, axis=0).astype(np.float32))
    assert v.shape == (NVROWS, 128)
    return v


def make_in_maps(inp, ncores=NCORES):
    f = lambda a: np.ascontiguousarray(np.asarray(a, dtype=np.float32))
    inp = {k: f(v) for k, v in inp.items()}
    st = _static_tables()
    vecs = _pack_vecs(inp)
    relb = np.ascontiguousarray(np.broadcast_to(inp["rel_bias"].reshape(1, 128), (128, 128)))
    sinkc = np.zeros((128, DEPTH * 2), np.float32)
    for l in range(DEPTH):
        for p in range(2):
            sinkc[0:64, l * 2 + p] = inp["attn_sinks"][l, p]
            sinkc[64:128, l * 2 + p] = inp["attn_sinks"][l, 2 + p]
    maps = []
    for c in range(ncores):
        b = c % 4
        ss = slice(c * NSS, (c + 1) * NSS)
        m = {
            "xp": inp["x_prompt"][b],
            "xs": inp["x_sample"][ss].reshape(NSS * TS, D),
            "cc": inp["cache_conv"][:, ss].reshape(DEPTH, NSS * 30, 256),
            "sh": inp["state_hgrn"][:, ss],
            "ck": inp["cache_swa_k"][:, ss].reshape(DEPTH, NSS, 128, 128),
            "cv": inp["cache_swa_v"][:, ss].reshape(DEPTH, NSS, 128, 128),
            "cvec": np.concatenate([inp["c_prompt"][b:b + 1], inp["c_sample"][ss]], axis=0),
            "relb": relb, "sinkc": sinkc, "vecs": vecs,
            "w_ada": inp["w_ada"], "w_in": inp["w_in"], "w_out": inp["w_out"], "w_up": inp["w_up"], "w_down": inp["w_down"],
        }
        for k in ("ident", "ohp", "ohsc", "ohsn", "mAp", "mAs", "rstp", "rsts", "seqoh", "cmask"):
            a = st[k]
            m[k] = np.ascontiguousarray(a.reshape(a.shape[0], -1))
        maps.append({k: np.ascontiguousarray(v) for k, v in m.items()})
    return maps


def kernel(**inputs):
    nc = _get_nc()
    maps = make_in_maps(inputs)
    res = run_bass_kernel_spmd(nc, maps, core_ids=list(range(NCORES)))
    r = res.results
    B = 4
    y_prompt = np.stack([r[b]["yp"] for b in range(B)]).reshape(B, SEQ, D)
    y_sample = np.concatenate([r[c]["ys"].reshape(NSS, TS, D) for c in range(NCORES)], axis=0)
    ncp = np.stack([r[b]["ncp"] for b in range(B)], axis=1)
    ncs = np.concatenate([r[c]["ncs"] for c in range(NCORES)], axis=1)
    nhp = np.stack([r[b]["nhp"] for b in range(B)], axis=1)
    nhs = np.concatenate([r[c]["nhs"] for c in range(NCORES)], axis=1)
    nkp = np.stack([r[b]["nkp"] for b in range(B)], axis=1).reshape(DEPTH, B, 128, 2, 64)
    nks = np.concatenate([r[c]["nks"] for c in range(NCORES)], axis=1).reshape(DEPTH, NCORES * NSS, 128, 2, 64)
    nvp = np.stack([r[b]["nvp"] for b in range(B)], axis=1).reshape(DEPTH, B, 128, 2, 64)
    nvs = np.concatenate([r[c]["nvs"] for c in range(NCORES)], axis=1).reshape(DEPTH, NCORES * NSS, 128, 2, 64)
    outs = (y_prompt, y_sample, ncp, ncs, nhp, nhs, nkp, nks, nvp, nvs)
    return tuple(np.ascontiguousarray(a.astype(np.float32)) for a in outs)
```

```python
import contextlib
import numpy as np
import ml_dtypes
import concourse.bass as bass
import concourse.mybir as mybir
from concourse.bass_utils import run_bass_kernel_spmd

F32 = mybir.dt.float32
BF16 = mybir.dt.bfloat16
AF = mybir.ActivationFunctionType
ALU = mybir.AluOpType
AX = mybir.AxisListType

NCORES = 8
D = 1024
SEQ = 4096
NSS = 16
TS = 4
NT = 512
DEPTH = 2
EPS = 1e-6
EPOCH = 1000
HGRN_ON = True
STRICT_SAME_ENGINE = False


class Op:
    __slots__ = ("eng", "fn", "reads", "writes", "dma", "waits", "sig", "seq", "idx", "grp")

    def __init__(self, eng, fn, reads, writes, dma):
        self.eng, self.fn, self.reads, self.writes, self.dma = eng, fn, reads, writes, dma
        self.waits = []
        self.sig = False
        self.seq = -1
        self.grp = None


class Prog:
    ENGS = ("pe", "act", "dve", "pool", "sp")

    def __init__(self):
        self.ops = []

    def add(self, eng, fn, reads=(), writes=(), dma=None):
        op = Op(eng, fn, tuple(reads), tuple(writes), dma)
        op.idx = len(self.ops)
        self.ops.append(op)
        return op

    def schedule(self):
        last_w = {}
        readers = {}
        for op in self.ops:
            deps = {}

            def need(p, kind):
                if p.dma is not None:
                    if op.dma == p.dma and kind == "WAW" and op.grp == p.grp:
                        return
                    deps[p.idx] = p
                    return
                if p.eng == op.eng:
                    if op.dma is None:
                        if op.eng == "pe":
                            return
                        if kind != "RAW" and not STRICT_SAME_ENGINE:
                            return
                deps[p.idx] = p

            for r in op.reads:
                p = last_w.get(r)
                if p is not None:
                    need(p, "RAW")
                if isinstance(r, tuple) and r[0] == "ps":
                    for p in readers.get(r, ()):
                        if p.eng != op.eng:
                            need(p, "RAR")
            for w in op.writes:
                p = last_w.get(w)
                if p is not None:
                    need(p, "WAW")
                for p in readers.get(w, ()):
                    need(p, "WAR")
            best = {}
            for p in deps.values():
                if p.dma is not None:
                    best[("dma", p.idx)] = p
                else:
                    q = best.get(p.eng)
                    if q is None or q.idx < p.idx:
                        best[p.eng] = p
            op.waits = list(best.values())
            for p in op.waits:
                if p.dma is None:
                    p.sig = True
            for w in op.writes:
                last_w[w] = op
                readers[w] = []
            for r in op.reads:
                readers.setdefault(r, []).append(op)
        cnt = {e: 0 for e in self.ENGS}
        for op in self.ops:
            if op.sig:
                op.seq = cnt[op.eng]
                cnt[op.eng] += 1
        self.sigcount = cnt


def emit_program(nc, prog, final_streams):
    prog.schedule()
    with contextlib.ExitStack() as es:
        esem = {}
        for e in Prog.ENGS:
            n = (prog.sigcount[e] + EPOCH - 1) // EPOCH
            esem[e] = [es.enter_context(nc.semaphore(f"p_{e}{i}")) for i in range(n)]
        streams = {}
        for op in prog.ops:
            if op.dma is not None:
                st = streams.get(op.dma)
                if st is None:
                    st = [es.enter_context(nc.semaphore(f"d_{op.dma}")), 0]
                    streams[op.dma] = st
                st[1] += 16
                op.seq = st[1]
        setup_total = streams["setup"][1] if "setup" in streams else 0
        block = es.enter_context(nc.Block())
        by_eng = {e: [o for o in prog.ops if o.eng == e] for e in Prog.ENGS}

        def run(engh, ename):
            known = {}
            for op in by_eng[ename]:
                for p in op.waits:
                    if p.dma is not None:
                        sem = streams[p.dma][0]
                        val = setup_total if p.dma == "setup" else p.seq
                    else:
                        sem = esem[p.eng][p.seq // EPOCH]
                        val = p.seq % EPOCH + 1
                    k = id(sem)
                    if known.get(k, 0) >= val:
                        continue
                    known[k] = val
                    engh.wait_ge(sem, val)
                ins = op.fn(engh)
                if op.dma is not None:
                    ins.then_inc(streams[op.dma][0], 16)
                elif op.sig:
                    ins.then_inc(esem[ename][op.seq // EPOCH], 1)
            if ename == "sp":
                for s in final_streams:
                    if s in streams:
                        engh.wait_ge(streams[s][0], streams[s][1])

        @block.tensor
        def _(t):
            run(t, "pe")

        @block.scalar
        def _(a):
            run(a, "act")

        @block.vector
        def _(v):
            run(v, "dve")

        @block.gpsimd
        def _(g):
            run(g, "pool")

        @block.sync
        def _(s):
            run(s, "sp")


def _t5_bucket(rel):
    rel = np.asarray(rel)
    n = np.maximum(rel, 0)
    nf = np.maximum(n, 16).astype(np.float32)
    large = 16 + (np.log(nf / np.float32(16)) / np.float32(np.log(128 / 16)) * np.float32(16)).astype(np.int32)
    large = np.minimum(large, 31)
    return np.where(n < 16, n, large)


def _static_tables():
    t = {}
    t["ident"] = np.eye(128, dtype=np.float32)
    k = np.arange(128)[:, None]
    q = np.arange(128)[None, :]
    oh = np.zeros((128, 2, 32, 128), np.float32)
    rel_prev = q + 128 - k
    rel_own = q - k
    for w, rel in ((0, rel_prev), (1, rel_own)):
        valid = (rel >= 0) & (rel <= 128)
        b = _t5_bucket(rel)
        for bb in range(32):
            oh[:, w, bb, :] = (valid & (b == bb))
    t["ohp"] = oh.astype(ml_dtypes.bfloat16)
    j = np.arange(128)[:, None]
    tt = np.arange(TS)[None, :]
    rel = tt + 128 - j
    valid = (rel >= 0) & (rel <= 128)
    b = _t5_bucket(rel)
    ohs = np.zeros((128, 32, TS), np.float32)
    for bb in range(32):
        ohs[:, bb, :] = valid & (b == bb)
    t["ohsc"] = ohs.astype(ml_dtypes.bfloat16)
    n = NSS * TS
    kk = np.arange(n)[:, None]
    qq = np.arange(n)[None, :]
    same = (kk // TS) == (qq // TS)
    rel = (qq % TS) - (kk % TS)
    ohn = np.zeros((n, TS, n), np.float32)
    for bb in range(TS):
        ohn[:, bb, :] = same & (rel == bb)
    t["ohsn"] = ohn.astype(ml_dtypes.bfloat16)
    s = np.arange(128)[:, None]
    u = np.arange(128)[None, :]
    t["mAp"] = (((s // 32) == (u // 32)) & (s <= u)).astype(np.float32)
    s = np.arange(n)[:, None]
    u = np.arange(n)[None, :]
    t["mAs"] = (((s // TS) == (u // TS)) & (s <= u)).astype(np.float32)
    rp = np.ones((128, NT), np.float32)
    rp[:, ::32] = 0
    t["rstp"] = rp
    rs = np.ones((128, n), np.float32)
    rs[:, ::TS] = 0
    t["rsts"] = rs
    t["cmask"] = ((np.arange(128)[:, None] // 32) == np.arange(4)[None, :]).astype(np.float32)
    t["seqoh"] = ((np.arange(n)[:, None] // TS) == np.arange(NSS)[None, :]).astype(np.float32)
    return t


VROWS_PER_LAYER = 48 + 8 + 8 + 2 + 2 + 2 + 4 + 1 + 62
V_BADA, V_GMIX, V_GMLP, V_CB, V_LNG, V_LNB, V_LB, V_HG, V_CW = 0, 48, 56, 64, 66, 68, 70, 74, 75
V_FINAL = 2 * VROWS_PER_LAYER
NVROWS = V_FINAL + 8


def bcast(ap, n):
    return bass.AP(tensor=ap.tensor, offset=ap.offset, ap=[list(x) for x in ap.ap] + [[0, n]])


def bcast_mid(ap, n):
    a = [list(x) for x in ap.ap]
    return bass.AP(tensor=ap.tensor, offset=ap.offset, ap=[a[0], [0, n]] + a[1:])


class Builder:
    def __init__(self, stage=99):
        self.stage = stage
        self.nc = bass.Bass("TRN2", target_bir_lowering=False)
        self.P = Prog()
        self.es = contextlib.ExitStack()
        self.ps_rr = 0

    def dram_in(self, name, shape, dt=F32):
        return self.nc.dram_tensor(name, list(shape), dt, kind="ExternalInput").ap()

    def dram_out(self, name, shape, dt=F32):
        return self.nc.dram_tensor(name, list(shape), dt, kind="ExternalOutput").ap()

    def sb(self, name, shape, dt=F32):
        return self.es.enter_context(self.nc.sbuf_tensor(name, list(shape), dt))

    def psum(self, name, shape, dt=F32):
        return self.es.enter_context(self.nc.psum_tensor(name, list(shape), dt))

    def ring_ps(self, nring=4):
        i = self.ps_rr % nring
        self.ps_rr += 1
        return i

    def act(self, out, in_, func, reads, writes, bias=None, scale=None):
        kw = {}
        if bias is not None:
            kw["bias"] = bias
        if scale is not None:
            kw["scale"] = scale
        self.P.add("act", lambda e: e.activation(out=out, in_=in_, func=func, **kw), reads, writes)

    def tt(self, out, a, b, op, reads, writes, eng="dve"):
        self.P.add(eng, lambda e: e.tensor_tensor(out=out, in0=a, in1=b, op=op), reads, writes)

    def ts(self, out, a, s1, op0, reads, writes, s2=None, op1=None, eng="dve"):
        if op1 is None:
            self.P.add(eng, lambda e: e.tensor_scalar(out=out, in0=a, scalar1=s1, scalar2=None, op0=op0), reads, writes)
        else:
            self.P.add(eng, lambda e: e.tensor_scalar(out=out, in0=a, scalar1=s1, scalar2=s2, op0=op0, op1=op1), reads, writes)

    def stt(self, out, a, s, b, op0, op1, reads, writes, eng="dve"):
        self.P.add(eng, lambda e: e.scalar_tensor_tensor(out=out, in0=a, scalar=s, in1=b, op0=op0, op1=op1), reads, writes)

    def copy(self, out, in_, reads, writes, eng="dve"):
        if eng == "act":
            self.P.add("act", lambda e: e.copy(out=out, in_=in_), reads, writes)
        else:
            self.P.add(eng, lambda e: e.tensor_copy(out=out, in_=in_), reads, writes)

    def dma(self, eng, out, in_, stream, reads, writes, grp=None, **kw):
        op = self.P.add(eng, lambda e: e.dma_start(out=out, in_=in_, **kw), reads, writes, dma=stream)
        if grp is None and stream == "cin":
            grp = ("cin", op.idx)
        op.grp = grp

    def mm_group(self, outs_ins, reads, writes):
        lst = list(outs_ins)
        import os, traceback
        dbg = os.environ.get("DEBUGMM")
        where = traceback.extract_stack()[-2].lineno if dbg else None

        def fn(e):
            ins = None
            for tup in lst:
                (o, l, r, st, sp) = tup[:5]
                if len(tup) > 5:
                    ins = e.matmul(o, lhsT=l, rhs=r, start=st, stop=sp, tile_position=tup[5])
                else:
                    ins = e.matmul(o, lhsT=l, rhs=r, start=st, stop=sp)
                if dbg:
                    print("MMDBG", where, str(ins)[:50].replace("\n"," "))
            return ins
        self.P.add("pe", fn, reads, writes)

    def transpose_f32(self, out, in_, kparts, reads, writes):
        idn = self.ident[0:kparts, 0:kparts]
        self.P.add("pe", lambda e: e.matmul(out, lhsT=in_, rhs=idn, start=True, stop=True), reads + ["ident"], writes)

    def build(self):
        nc, P = self.nc, self.P
        n = NSS * TS
        d = {}
        d["xp"] = self.dram_in("xp", [SEQ, D])
        d["xs"] = self.dram_in("xs", [n, D])
        d["cc"] = self.dram_in("cc", [DEPTH, NSS * 30, 256])
        d["sh"] = self.dram_in("sh", [DEPTH, NSS, 4, 128, 128])
        d["ck"] = self.dram_in("ck", [DEPTH, NSS, 128, 128])
        d["cv"] = self.dram_in("cv", [DEPTH, NSS, 128, 128])
        d["cvec"] = self.dram_in("cvec", [1 + NSS, D])
        d["relb"] = self.dram_in("relb", [128, 128])
        d["sinkc"] = self.dram_in("sinkc", [128, DEPTH * 2])
        d["vecs"] = self.dram_in("vecs", [NVROWS, 128])
        d["w_ada"] = self.dram_in("w_ada", [DEPTH, D, 6 * D])
        d["w_in"] = self.dram_in("w_in", [DEPTH, D, 3072])
        d["w_out"] = self.dram_in("w_out", [DEPTH, D, D])
        d["w_up"] = self.dram_in("w_up", [DEPTH, D, 4 * D])
        d["w_down"] = self.dram_in("w_down", [DEPTH, 4 * D, D])
        st = _static_tables()
        d["ident"] = self.dram_in("ident", [128, 128])
        d["ohp"] = self.dram_in("ohp", [128, 2 * 32 * 128], BF16)
        d["ohsc"] = self.dram_in("ohsc", [128, 32 * TS], BF16)
        d["ohsn"] = self.dram_in("ohsn", [n, TS * n], BF16)
        d["mAp"] = self.dram_in("mAp", [128, 128])
        d["mAs"] = self.dram_in("mAs", [n, n])
        d["rstp"] = self.dram_in("rstp", [128, NT])
        d["rsts"] = self.dram_in("rsts", [128, n])
        d["seqoh"] = self.dram_in("seqoh", [n, NSS])
        d["cmask"] = self.dram_in("cmask", [128, 4])
        o = {}
        o["yp"] = self.dram_out("yp", [SEQ, D])
        o["ys"] = self.dram_out("ys", [n, D])
        o["ncp"] = self.dram_out("ncp", [DEPTH, 30, 256])
        o["ncs"] = self.dram_out("ncs", [DEPTH, NSS, 30, 256])
        o["nhp"] = self.dram_out("nhp", [DEPTH, 4, 128, 128])
        o["nhs"] = self.dram_out("nhs", [DEPTH, NSS, 4, 128, 128])
        o["nkp"] = self.dram_out("nkp", [DEPTH, 128, 128])
        o["nks"] = self.dram_out("nks", [DEPTH, NSS, 128, 128])
        o["nvp"] = self.dram_out("nvp", [DEPTH, 128, 128])
        o["nvs"] = self.dram_out("nvs", [DEPTH, NSS, 128, 128])
        self.d, self.o = d, o

        self.ident = self.sb("ident_sb", [128, 128])
        self.identb = self.sb("identb", [128, 128], BF16)
        self.onesb = self.sb("onesb", [128, 128], BF16)
        self.onesLN = self.sb("onesLN", [128, 128])
        self.epsc = self.sb("epsc", [128, 1])
        self.vecT = self.sb("vecT", [128, NVROWS])
        self.NSLOT = 4
        self.wring = [self.sb(f"wr{i}", [128, 8192], BF16) for i in range(self.NSLOT)]
        self.x = self.sb("x", [128, 8, NT])
        self.h = self.sb("h", [128, 8, NT], BF16)
        self.a_ext = self.sb("a_ext", [128, DEPTH, 2, 30 + NT], BF16)
        self.a_exts = self.sb("a_exts", [128, 2, NSS, 30 + TS], BF16)
        self.kTa = self.sb("kTa", [128, DEPTH, 128 + NT], BF16)
        self.vatm = self.sb("vatm", [128, DEPTH, 5, 2, 128], BF16)
        self.onespad = self.sb("onespad", [128, 2, 128], BF16)
        self.S = self.sb("S", [128, DEPTH, 4, 128])
        self.mod = self.sb("mod", [128, DEPTH, 48, 1 + NSS])
        self.diagw = self.sb("diagw", [128, 2, 31, 128], BF16)
        self.EBp = self.sb("EBp", [128, 2, 4, 128])
        self.EBsc = self.sb("EBsc", [128, 4, TS])
        self.EBsn = self.sb("EBsn", [n, 4, n])
        self.expT = self.sb("expT", [128, 128])
        self.lbv = self.sb("lbv", [128, DEPTH, 3, 4])
        self.esink = self.sb("esink", [128, DEPTH * 2])
        self.mAp = self.sb("mAp_sb", [128, 128])
        self.mAs = self.sb("mAs_sb", [n, n])
        self.rstp = self.sb("rstp_sb", [128, NT])
        self.rsts = self.sb("rsts_sb", [128, n])
        self.seqoh = self.sb("seqoh_sb", [n, NSS])
        self.cmask = self.sb("cmask_sb", [128, 4])
        self.csc = self.sb("csc", [128, 8, 1 + NSS], BF16)
        self.chs = self.sb("chs", [128, 4, 16])
        self.wt = self.sb("wt", [128, 2, 128])
        self.ostg = self.sb("ostg", [128, DEPTH, 2, 256])
        self.ostgs = self.sb("ostgs", [64, DEPTH, 2, 256])
        self.am = self.sb("am", [128, 4, 128], BF16)
        self.NPG = 23
        self.arena = self.sb("arena", [128, self.NPG * 512])
        self.ps = [self.psum(f"ps{i}", [128, 512]) for i in range(8)]
        self.wcount = 0

        self.setup()
        import os
        ntile = int(os.environ.get("NTILE", SEQ // NT))
        if self.stage < 50:
            ntile = 1
        self.ntile = ntile
        for t in range(ntile):
            self.tile_prompt(t, last=(t == ntile - 1) and self.stage >= 50 and not os.environ.get("NOLAST"))
        if self.stage >= 60:
            self.tile_sample()
        emit_program(nc, P, ["outy0", "outy1", "outm", "outh"])
        return nc

    def pg(self, p0, npg=1):
        return [("a", p) for p in range(p0, p0 + npg)]

    def av(self, p0, npg, dt=F32):
        v = self.arena[:, p0 * 512:(p0 + npg) * 512]
        if dt == BF16:
            v = v.bitcast(BF16)
        return v

    def psk(self, i):
        return [("ps", i)]

    def wplan_init(self, plan):
        self.wplan = plan
        self.wnext = 0
        self.wuse = 0
        for _ in range(self.NSLOT):
            self.w_record()

    def w_record(self):
        if self.wnext < len(self.wplan):
            slot = self.wnext % self.NSLOT
            self.wplan[self.wnext](slot)
            self.wnext += 1

    def w_acquire(self):
        slot = self.wuse % self.NSLOT
        self.wuse += 1
        return slot

    def w_release(self):
        self.w_record()

    def wload(self, slot, dst_sl, src, part=None):
        t = self.wring[slot]
        dst = dst_sl(t)
        stream = f"w{slot}_{(self.wnext // self.NSLOT) % 3}"
        nk = dst.shape[1] if len(dst.shape) == 3 else 1
        if len(dst.shape) == 3 and dst.shape[0] == 128 and nk >= 8 and dst.shape[2] >= 256:
            step = 4
            for k0 in range(0, nk, step):
                self.dma("pool", dst[:, k0:k0 + step, :], src[:, k0:k0 + step, :], stream, [], [("w", slot)], grp=self.wnext)
        else:
            self.dma("pool", dst, src, stream, [], [("w", slot)], grp=self.wnext)

    def plan_weights(self, ntiles_total):
        d = self.d
        plan = []
        for l in range(DEPTH):
            wa = d["w_ada"][l].rearrange("(k p) n -> p k n", p=128)
            for j in range(6):
                def f(slot, wa=wa, j=j):
                    self.wload(slot, lambda t: t[:, :].rearrange("p (k n) -> p k n", k=8), wa[:, :, j * 1024:(j + 1) * 1024])
                plan.append(f)
        for _ in range(ntiles_total):
            for l in range(DEPTH):
                wi = d["w_in"][l].rearrange("(k p) n -> p k n", p=128)
                for j in range(2):
                    def f(slot, wi=wi, j=j):
                        self.wload(slot, lambda t: t[:, :].rearrange("p (k n) -> p k n", k=8), wi[:, :, j * 1024:(j + 1) * 1024])
                    plan.append(f)

                def f2(slot, wi=wi):
                    v = lambda t: t[:, :].rearrange("p (k n) -> p k n", k=8)
                    self.wload(slot, lambda t: v(t)[:, :, 0:512], wi[:, :, 2048:2560])
                    for pos, hh in enumerate((0, 2, 1, 3)):
                        self.wload(slot, lambda t, pos=pos: v(t)[:, :, 512 + pos * 64:512 + (pos + 1) * 64],
                                   wi[:, :, 2560 + hh * 64:2560 + (hh + 1) * 64])
                    self.wload(slot, lambda t: v(t)[:, :, 768:1024], wi[:, :, 2816:3072])
                plan.append(f2)
                wo = d["w_out"][l]

                def f3(slot, wo=wo):
                    v = lambda t: t[:, :].rearrange("p (k n) -> p k n", k=8)
                    self.wload(slot, lambda t: v(t)[:, 0:6, :], wo[0:768, :].rearrange("(k p) n -> p k n", p=128))
                    for kk, (ha, hb) in ((6, (0, 2)), (7, (1, 3))):
                        self.wload(slot, lambda t, kk=kk: v(t)[0:64, kk, :], wo[768 + ha * 64:768 + (ha + 1) * 64, :])
                        self.wload(slot, lambda t, kk=kk: v(t)[64:128, kk, :], wo[768 + hb * 64:768 + (hb + 1) * 64, :])
                plan.append(f3)
                wu = d["w_up"][l].rearrange("(k p) n -> p k n", p=128)
                for j in range(4):
                    def f(slot, wu=wu, j=j):
                        self.wload(slot, lambda t: t[:, :].rearrange("p (k n) -> p k n", k=8), wu[:, :, j * 1024:(j + 1) * 1024])
                    plan.append(f)
                wd = d["w_down"][l].rearrange("(k p) n -> p k n", p=128)
                for j in range(4):
                    def f(slot, wd=wd, j=j):
                        self.wload(slot, lambda t: t[:, :].rearrange("p (k n) -> p k n", k=32), wd[:, :, j * 256:(j + 1) * 256])
                    plan.append(f)
        return plan

    def setup(self):
        import os
        P, d = self.P, self.d
        n = NSS * TS
        A = self.arena
        sp = "sp"
        for (dst, src, key) in ((self.ident[:, :], d["ident"], "ident"), (self.mAp[:, :], d["mAp"], "mAp"),
                                (self.mAs[:, :], d["mAs"], "mAs"), (self.rstp[:, :], d["rstp"], "rstp"),
                                (self.rsts[:, :], d["rsts"], "rsts"), (self.seqoh[:, :], d["seqoh"], "seqoh"), (self.cmask[:, :], d["cmask"], "cmask"),
                                (self.expT[:, :], d["relb"], "expT"), (self.esink[:, :], d["sinkc"], "esink")):
            self.dma(sp, dst, src, "setup", [], [key])
        nblk = (NVROWS + 127) // 128
        for r in range(nblk):
            rows = min(128, NVROWS - r * 128)
            self.dma(sp, A[0:rows, r * 128:(r + 1) * 128], d["vecs"][r * 128:r * 128 + rows, :], "setup", [], self.pg(0))
        self.dma(sp, A[0:1 + NSS, 512:512 + D], d["cvec"], "setup", [], self.pg(1, 2))
        ohp = self.av(4, 8, BF16)
        self.dma(sp, ohp, d["ohp"], "setup", [], self.pg(4, 8))
        ohsc = self.av(20, 1, BF16)[:, 0:32 * TS]
        self.dma(sp, ohsc, d["ohsc"], "setup", [], self.pg(20))
        ohsn = self.av(21, 1, BF16)[0:n, 0:TS * n]
        self.dma(sp, ohsn, d["ohsn"], "setup", [], self.pg(21))
        ms = lambda ap, v, w, eng="dve": P.add(eng, lambda e: e.memset(ap, v), [], w)
        ms(self.onesb[:, :], 1.0, ["onesb"])
        ms(self.onesLN[:, :], 1.0 / 256.0, ["onesLN"])
        ms(self.epsc[:, :], EPS, ["epsc"])
        ms(self.a_ext[:, :, :, :], 0.0, ["a_ext0", "a_ext1"])
        ms(self.kTa[:, :, :], 0.0, ["kTa0", "kTa1"])
        ms(self.vatm[:, :, :, :, :], 0.0, ["vatm0", "vatm1"])
        ms(self.onespad[:, :, :], 0.0, ["onespad"])
        ms(self.onespad[:, 0, 0:64], 1.0, ["onespad"])
        ms(self.onespad[:, 1, 64:128], 1.0, ["onespad"])
        ms(self.S[:, :, :, :], 0.0, ["S0", "S1"])
        ms(self.lbv[:, :, :, :], 0.0, ["lbv"])
        self.copy(self.identb[:, :], self.ident[:, :], ["ident"], ["identb"])
        for r in range(nblk):
            rows = min(128, NVROWS - r * 128)
            self.transpose_f32(self.ps[r][:, 0:rows], A[0:rows, r * 128:(r + 1) * 128], rows, self.pg(0), self.psk(r))
            self.copy(self.vecT[:, r * 128:r * 128 + rows], self.ps[r][:, 0:rows], self.psk(r), ["vecT"])
        nb = 1 + NSS
        for k in range(8):
            self.transpose_f32(self.ps[4][:, k * nb:(k + 1) * nb], A[0:nb, 512 + k * 128:512 + (k + 1) * 128], nb,
                               self.pg(1, 2), self.psk(4))
        self.act(self.csc[:, :, :], self.ps[4][:, 0:8 * nb].rearrange("p (k c) -> p k c", k=8), AF.Silu, self.psk(4), ["csc"])
        self.act(self.expT[:, :], self.expT[:, :], AF.Exp, ["expT"], ["expT"])
        self.act(self.esink[:, :], self.esink[:, :], AF.Exp, ["esink"], ["esink"])
        ohp3 = ohp.rearrange("p (w b q) -> p w b q", w=2, b=32)
        for w in range(2):
            for hs_, hh in enumerate((0, 2, 1, 3)):
                dst = self.EBp[:, w, hs_, :]
                for b in range(32):
                    sc = self.expT[:, b * 4 + hh:b * 4 + hh + 1]
                    if b == 0:
                        self.ts(dst, ohp3[:, w, b, :], sc, ALU.mult, self.pg(4, 8) + ["expT"], ["EBp"])
                    else:
                        self.stt(dst, ohp3[:, w, b, :], sc, dst, ALU.mult, ALU.add, self.pg(4, 8) + ["expT", "EBp"], ["EBp"])
        ohsc3 = ohsc.rearrange("p (b t) -> p b t", b=32)
        ohsn3 = ohsn.rearrange("p (b t) -> p b t", b=TS)
        for hs_, hh in enumerate((0, 2, 1, 3)):
            dst = self.EBsc[:, hs_, :]
            for b in range(32):
                sc = self.expT[:, b * 4 + hh:b * 4 + hh + 1]
                if b == 0:
                    self.ts(dst, ohsc3[:, b, :], sc, ALU.mult, self.pg(20) + ["expT"], ["EBsc"])
                else:
                    self.stt(dst, ohsc3[:, b, :], sc, dst, ALU.mult, ALU.add, self.pg(20) + ["expT", "EBsc"], ["EBsc"])
            dst = self.EBsn[:, hs_, :]
            for b in range(TS):
                sc = self.expT[0:n, b * 4 + hh:b * 4 + hh + 1]
                if b == 0:
                    self.ts(dst, ohsn3[:, b, :], sc, ALU.mult, self.pg(21) + ["expT"], ["EBsn"])
                else:
                    self.stt(dst, ohsn3[:, b, :], sc, dst, ALU.mult, ALU.add, self.pg(21) + ["expT", "EBsn"], ["EBsn"])
        VR = VROWS_PER_LAYER
        c0 = self.vecT[:, V_LB:V_LB + 4]
        c1 = self.vecT[:, VR + V_LB:VR + V_LB + 4]
        self.tt(self.lbv[:, 1, 0, :], c1, c0, ALU.subtract, ["vecT"], ["lbv"])
        self.act(self.lbv[:, 1, 0, :], self.lbv[:, 1, 0, :], AF.Sigmoid, ["lbv"], ["lbv"])
        for l in range(DEPTH):
            self.ts(self.lbv[:, l, 1, :], self.lbv[:, l, 0, :], -1.0, ALU.mult, ["lbv"], ["lbv"], 1.0, ALU.add)
            self.ts(self.lbv[:, l, 2, :], self.lbv[:, l, 1, :], -1.0, ALU.mult, ["lbv"], ["lbv"])
        ntiles_total = (int(os.environ.get("NTILE", SEQ // NT)) if self.stage >= 50 else 1) + (1 if self.stage >= 60 else 0)
        self.wplan_init(self.plan_weights(ntiles_total))
        for l in range(DEPTH):
            for j6 in range(6):
                slot = self.w_acquire()
                wv = self.wring[slot][:, :].rearrange("p (k n) -> p k n", k=8)
                for jj in range(8):
                    j = j6 * 8 + jj
                    bank = 5 + (j // 24)
                    col = (j % 24) * nb
                    self.mm_group([(self.ps[bank][:, col:col + nb], wv[:, k, jj * 128:(jj + 1) * 128], self.csc[:, k, :], k == 0, k == 7)
                                   for k in range(8)], [("w", slot), "csc"], self.psk(bank))
                self.w_release()
            for half in range(2):
                bank = 5 + half
                self.tt(self.mod[:, l, half * 24:(half + 1) * 24, :],
                        self.ps[bank][:, 0:24 * nb].rearrange("p (j c) -> p j c", j=24),
                        bcast(self.vecT[:, l * VR + V_BADA + half * 24:l * VR + V_BADA + (half + 1) * 24], nb),
                        ALU.add, self.psk(bank) + ["vecT"], [("mod", l)])
            for (c0_, gcol) in ((8, V_GMIX), (32, V_GMLP)):
                m = self.mod[:, l, c0_:c0_ + 8, :]
                self.ts(m, m, 1.0, ALU.add, [("mod", l)], [("mod", l)])
                self.tt(m, m, bcast(self.vecT[:, l * VR + gcol:l * VR + gcol + 8], nb), ALU.mult, [("mod", l), "vecT"], [("mod", l)])

    def rmsnorm_mod(self, N, l, mul_c0, add_c0, cols, xkeys, grp):
        x, h = self.x, self.h
        sq = self.av(4, 4, BF16).rearrange("p (k n) -> p k n", k=8)
        for k in range(8):
            self.act(sq[:, k, 0:N], x[:, k, 0:N], AF.Square, [("x", k)], self.pg(4 + k // 2))
        bank = self.ring_ps()
        self.mm_group([(self.ps[bank][:, 0:N], self.onesb[:, :], sq[:, k, 0:N], k == 0, k == 7) for k in range(8)],
                      self.pg(4, 4) + ["onesb"], self.psk(bank))
        lnv = self.av(8, 1)[:, 0:N]
        rstd = self.av(9, 1)[:, 0:N]
        self.act(lnv, self.ps[bank][:, 0:N], AF.Ln, self.psk(bank) + ["epsc"], self.pg(8), bias=self.epsc[:, 0:1], scale=1.0 / D)
        self.act(rstd, lnv, AF.Exp, self.pg(8), self.pg(9), scale=-0.5)
        for k in range(8):
            hn = self.av(10 + (k % 2), 1)[:, 0:N]
            if grp == "p":
                self.stt(hn, x[:, k, 0:N], self.mod[:, l, mul_c0 + k, 0:1], rstd, ALU.mult, ALU.mult,
                         [("x", k), ("mod", l)] + self.pg(9), self.pg(10 + (k % 2)))
                self.act(h[:, k, 0:N], hn, AF.Identity, self.pg(10 + (k % 2)) + [("mod", l)], [("h", k)],
                         bias=self.mod[:, l, add_c0 + k, 0:1])
            else:
                v3 = lambda a: a.rearrange("p (s t) -> p s t", t=TS)
                self.tt(hn, x[:, k, 0:N], rstd, ALU.mult, [("x", k)] + self.pg(9), self.pg(10 + (k % 2)))
                self.tt(v3(hn), v3(hn), bcast(self.mod[:, l, mul_c0 + k, 1:1 + NSS], TS), ALU.mult,
                        self.pg(10 + (k % 2)) + [("mod", l)], self.pg(10 + (k % 2)))
                self.tt(v3(h[:, k, 0:N]), v3(hn), bcast(self.mod[:, l, add_c0 + k, 1:1 + NSS], TS), ALU.add,
                        self.pg(10 + (k % 2)) + [("mod", l)], [("h", k)])

    def gated_residual(self, N, l, j, bank, gate_c0, grp):
        x = self.x
        if grp == "p":
            self.stt(x[:, j, 0:N], self.ps[bank][:, 0:N], self.mod[:, l, gate_c0 + j, 0:1], x[:, j, 0:N], ALU.mult, ALU.add,
                     self.psk(bank) + [("x", j), ("mod", l)], [("x", j)])
        else:
            tmp = self.av(22, 1)[:, (j % 2) * 256:(j % 2) * 256 + N]
            v3 = lambda a: a.rearrange("p (s t) -> p s t", t=TS)
            self.tt(v3(tmp), v3(self.ps[bank][:, 0:N]), bcast(self.mod[:, l, gate_c0 + j, 1:1 + NSS], TS), ALU.mult,
                    self.psk(bank) + [("mod", l)], self.pg(22))
            self.tt(x[:, j, 0:N], x[:, j, 0:N], tmp, ALU.add, [("x", j)] + self.pg(22), [("x", j)])

    def hkeys(self):
        return [("h", k) for k in range(8)]

    def mlp(self, N, l, grp):
        h = self.h
        self.rmsnorm_mod(N, l, 32, 24, None, None, grp)
        hid = self.av(4, 16, BF16).rearrange("p (k n) -> p k n", k=32)
        for j4 in range(4):
            slot = self.w_acquire()
            wv = self.wring[slot][:, :].rearrange("p (k n) -> p k n", k=8)
            for jj in range(8):
                j = j4 * 8 + jj
                bank = self.ring_ps(8)
                self.mm_group([(self.ps[bank][:, 0:N], wv[:, k, jj * 128:(jj + 1) * 128], h[:, k, 0:N], k == 0, k == 7)
                               for k in range(8)], [("w", slot)] + self.hkeys(), self.psk(bank))
                rl = self.av(20 + (j % 2), 1)[:, 0:N]
                self.act(rl, self.ps[bank][:, 0:N], AF.Relu, self.psk(bank), self.pg(20 + (j % 2)))
                self.tt(hid[:, j, 0:N], rl, rl, ALU.mult, self.pg(20 + (j % 2)), [("hid", j)] + self.pg(4 + j // 2))
            self.w_release()
        for j4 in range(4):
            slot = self.w_acquire()
            wv = self.wring[slot][:, :].rearrange("p (k n) -> p k n", k=32)
            for jj in range(2):
                j = j4 * 2 + jj
                bank = self.ring_ps(8)
                self.mm_group([(self.ps[bank][:, 0:N], wv[:, k, jj * 128:(jj + 1) * 128], hid[:, k, 0:N], k == 0, k == 31)
                               for k in range(32)], [("w", slot)] + self.pg(4, 16), self.psk(bank))
                self.gated_residual(N, l, j, bank, 40, grp)
            self.w_release()

    def load_x(self, src_rows, nblk, rows_per_blk):
        for b in range(nblk):
            stg = self.av(4 + 2 * (b % 2), 2)
            self.dma("sp", stg[0:rows_per_blk, :], src_rows[b * rows_per_blk:(b + 1) * rows_per_blk, :], f"xin{b % 2}",
                     [], self.pg(4 + 2 * (b % 2), 2))
            for k4 in range(2):
                bank = self.ring_ps(8)
                for kk in range(4):
                    k = k4 * 4 + kk
                    self.transpose_f32(self.ps[bank][:, kk * 128:kk * 128 + rows_per_blk], stg[0:rows_per_blk, k * 128:(k + 1) * 128],
                                       rows_per_blk, self.pg(4 + 2 * (b % 2), 2), self.psk(bank))
                dst = self.x[:, k4 * 4:(k4 + 1) * 4, b * rows_per_blk:(b + 1) * rows_per_blk]
                src = self.ps[bank][:, :].rearrange("p (k n) -> p k n", k=4)[:, :, 0:rows_per_blk]
                self.copy(dst, src, self.psk(bank), [("x", k4 * 4 + i) for i in range(4)], eng=("act" if k4 else "dve"))

    def final_store(self, N, dst_rows, nblk, rows_per_blk, grp):
        x = self.x
        sq = self.av(4, 4, BF16).rearrange("p (k n) -> p k n", k=8)
        for k in range(8):
            self.act(sq[:, k, 0:N], x[:, k, 0:N], AF.Square, [("x", k)], self.pg(4 + k // 2))
        bank = self.ring_ps(8)
        self.mm_group([(self.ps[bank][:, 0:N], self.onesb[:, :], sq[:, k, 0:N], k == 0, k == 7) for k in range(8)],
                      self.pg(4, 4) + ["onesb"], self.psk(bank))
        lnv = self.av(8, 1)[:, 0:N]
        rstd = self.av(9, 1)[:, 0:N]
        self.act(lnv, self.ps[bank][:, 0:N], AF.Ln, self.psk(bank) + ["epsc"], self.pg(8), bias=self.epsc[:, 0:1], scale=1.0 / D)
        self.act(rstd, lnv, AF.Exp, self.pg(8), self.pg(9), scale=-0.5)
        yt = self.av(10, 8).rearrange("p (k n) -> p k n", k=8)
        for k in range(8):
            self.stt(yt[:, k, 0:N], x[:, k, 0:N], self.vecT[:, V_FINAL + k:V_FINAL + k + 1], rstd, ALU.mult, ALU.mult,
                     [("x", k), "vecT"] + self.pg(9), self.pg(10 + k))
        for b in range(nblk):
            stg = self.av(18 + 2 * (b % 2), 2)
            for k4 in range(2):
                bank = self.ring_ps(8)
                for kk in range(4):
                    k = k4 * 4 + kk
                    self.P.add("pe", lambda e, bank=bank, kk=kk, k=k, b=b: e.matmul(
                        self.ps[bank][0:rows_per_blk, kk * 128:(kk + 1) * 128], lhsT=yt[:, k, b * rows_per_blk:(b + 1) * rows_per_blk],
                        rhs=self.ident[:, :], start=True, stop=True), self.pg(10 + k) + ["ident"], self.psk(bank))
                self.copy(stg[0:rows_per_blk, k4 * 512:(k4 + 1) * 512], self.ps[bank][0:rows_per_blk, :], self.psk(bank),
                          self.pg(18 + 2 * (b % 2), 2), eng=("act" if k4 else "dve"))
            self.dma("sp", dst_rows[b * rows_per_blk:(b + 1) * rows_per_blk, :], stg[0:rows_per_blk, :], f"outy{b % 2}",
                     self.pg(18 + 2 * (b % 2), 2), [])

    def fm_proj(self, slot, col0, N, nring=4):
        wv = self.wring[slot][:, :].rearrange("p (k n) -> p k n", k=8)
        bank = self.ring_ps(nring)
        self.mm_group([(self.ps[bank][:, 0:N], wv[:, k, col0:col0 + 128], self.h[:, k, 0:N], k == 0, k == 7) for k in range(8)],
                      [("w", slot)] + self.hkeys(), self.psk(bank))
        return bank

    def build_diagw(self, l):
        VR = VROWS_PER_LAYER
        for c in range(2):
            for k in range(31):
                col = l * VR + V_CW + k * 2 + c
                self.ts(self.diagw[:, c, k, :], self.identb[:, :], self.vecT[:, col:col + 1], ALU.mult, ["identb", "vecT"], ["diagw"])

    def conv_branch(self, N, l, s0, aext_view, aext_key, a32, ln_pages=True):
        VR = VROWS_PER_LAYER
        mix = self.av(0, 4, BF16).rearrange("p (k n) -> p k n", k=8)
        sg = self.av(6, 1)[:, 0:N]
        for c in range(2):
            bv = self.fm_proj(s0, c * 128, N)
            bg = self.fm_proj(s0, 256 + c * 128, N)
            self.act(sg, self.ps[bg][:, 0:N], AF.Sigmoid, self.psk(bg), self.pg(6))
            self.tt(a32[c], self.ps[bv][:, 0:N], sg, ALU.mult, self.psk(bv) + self.pg(6), self.pg(4 + c))
            self.copy(aext_view(c, None), a32[c] if N == NT else a32[c].rearrange("p (s t) -> p s t", t=TS), self.pg(4 + c), [aext_key], eng="act")
        import os
        CUT = int(os.environ.get("CUT", "9"))
        if CUT <= 1:
            for c in range(2):
                self.P.add("dve", lambda e, c=c: e.memset(mix[:, c, :], 0.0), [], self.pg(0))
            return
        self.build_diagw(l)
        ac = [self.av(7 + c, 1)[:, 0:N] for c in range(2)]
        acsq = [self.av(9 + c, 1)[:, 0:N] for c in range(2)]
        for c in range(2):
            bank = self.ring_ps(4)
            self.mm_group([(self.ps[bank][:, 0:N], self.diagw[:, c, k, :], aext_view(c, k), k == 0, k == 30) for k in range(31)],
                          ["diagw", aext_key], self.psk(bank))
            cb = self.vecT[:, l * VR + V_CB + c:l * VR + V_CB + c + 1]
            self.act(ac[c], self.ps[bank][:, 0:N], AF.Identity, self.psk(bank) + ["vecT"], self.pg(7 + c), bias=cb)
            self.act(acsq[c], self.ps[bank][:, 0:N], AF.Square, self.psk(bank) + ["vecT"], self.pg(9 + c), bias=cb)
        if CUT <= 2:
            for c in range(2):
                self.P.add("dve", lambda e, c=c: e.memset(mix[:, c, :], 0.0), [], self.pg(0))
            return
        bm = self.ring_ps(4)
        self.mm_group([(self.ps[bm][:, 0:N], self.onesLN[:, :], ac[c], c == 0, c == 1) for c in range(2)], ["onesLN"] + self.pg(7, 2), self.psk(bm))
        bq = self.ring_ps(4)
        self.mm_group([(self.ps[bq][:, 0:N], self.onesLN[:, :], acsq[c], c == 0, c == 1) for c in range(2)], ["onesLN"] + self.pg(9, 2), self.psk(bq))
        mean = self.av(11, 1)[:, 0:N]
        var = self.av(12, 1)[:, 0:N]
        rstdc = self.av(13, 1)[:, 0:N]
        self.copy(mean, self.ps[bm][:, 0:N], self.psk(bm), self.pg(11), eng="act")
        self.tt(var, mean, mean, ALU.mult, self.pg(11), self.pg(12))
        self.tt(var, self.ps[bq][:, 0:N], var, ALU.subtract, self.psk(bq) + self.pg(12), self.pg(12))
        self.act(rstdc, var, AF.Ln, self.pg(12) + ["epsc"], self.pg(13), bias=self.epsc[:, 0:1])
        self.act(rstdc, rstdc, AF.Exp, self.pg(13), self.pg(13), scale=-0.5)
        if CUT <= 3:
            for c in range(2):
                self.P.add("dve", lambda e, c=c: e.memset(mix[:, c, :], 0.0), [], self.pg(0))
            return
        for c in range(2):
            xc = self.av(14, 1)[:, 0:N]
            self.tt(xc, ac[c], mean, ALU.subtract, self.pg(7 + c) + self.pg(11), self.pg(14))
            self.stt(xc, xc, self.vecT[:, l * VR + V_LNG + c:l * VR + V_LNG + c + 1], rstdc, ALU.mult, ALU.mult,
                     self.pg(14) + self.pg(13) + ["vecT"], self.pg(14))
            self.act(mix[:, c, 0:N], xc, AF.Silu, self.pg(14) + ["vecT"], self.pg(0),
                     bias=self.vecT[:, l * VR + V_LNB + c:l * VR + V_LNB + c + 1])

    def mixer_prompt(self, t, l, last):
        import os
        N = NT
        VR = VROWS_PER_LAYER
        P = self.P
        h = self.h
        mix = self.av(0, 4, BF16).rearrange("p (k n) -> p k n", k=8)
        self.rmsnorm_mod(N, l, 8, 0, None, None, "p")
        s0, s1, s2 = self.w_acquire(), self.w_acquire(), self.w_acquire()
        wv1 = self.wring[s1][:, :].rearrange("p (k n) -> p k n", k=8)
        wv2 = self.wring[s2][:, :].rearrange("p (k n) -> p k n", k=8)
        a32 = [self.av(4 + c, 1)[:, 0:N] for c in range(2)]
        aext_key = f"a_ext{l}"

        def aext_view(c, k):
            if k is None:
                return self.a_ext[:, l, c, 30:30 + N]
            return self.a_ext[:, l, c, k:k + N]
        self.conv_branch(N, l, s0, aext_view, aext_key, a32)
        if last and not os.environ.get("NOLASTC"):
            bank = self.ring_ps(4)
            for c in range(2):
                self.transpose_f32(self.ps[bank][:, c * 128:(c + 1) * 128], a32[c][:, N - 128:N], 128, self.pg(4 + c), self.psk(bank))
            stg = self.ostg[:, l, 0, :]
            self.copy(stg, self.ps[bank][:, 0:256], self.psk(bank), [("ostg", l, 0)])
            if "a" in os.environ.get("LASTSEL", "abc"):
                self.dma("sp", self.o["ncp"][l], stg[98:128, :], "outm", [("ostg", l, 0)], [])
        self.copy(self.a_ext[:, l, :, 0:30], self.a_ext[:, l, :, N:N + 30], [aext_key], [aext_key])
        if self.stage < 4:
            for k in range(2, 8):
                P.add("dve", lambda e, k=k: e.memset(mix[:, k, :], 0.0), [], self.pg(k // 2))
        if self.stage >= 4:
            qm = self.av(4, 2, BF16).rearrange("p (k n) -> p k n", k=4)
            kkey, vkey = f"kTa{l}", f"vatm{l}"
            P.add("dve", lambda e: e.memset(qm[:, :, :], 0.0), [], self.pg(4, 2))
            for p in range(2):
                bk = self.fm_proj(s2, 512 + p * 128, N)
                for c in range(2):
                    self.act(qm[c * 64:(c + 1) * 64, p * 2 + c, :], self.ps[bk][c * 64:(c + 1) * 64, 0:N], AF.Identity, self.psk(bk), self.pg(4, 2), scale=0.125)
            bk = self.fm_proj(s2, 768, N)
            self.copy(self.kTa[:, l, 128:128 + N], self.ps[bk][:, 0:N], self.psk(bk), [kkey], eng="act")
            for b in range(4):
                bank = self.ring_ps(4)
                self.mm_group([(self.ps[bank][:, 0:256], h[:, k, b * 128:(b + 1) * 128], wv2[:, k, 768:1024], k == 0, k == 7) for k in range(8)],
                              [("w", s2)] + self.hkeys(), self.psk(bank))
                for c in range(2):
                    self.copy(self.vatm[:, l, 1 + b, c, c * 64:(c + 1) * 64], self.ps[bank][:, 128 + c * 64:128 + (c + 1) * 64], self.psk(bank), [vkey])
                if last and b == 3 and not os.environ.get("NOLASTA"):
                    stg = self.ostg[:, l, 1, :]
                    self.copy(stg, self.ps[bank][:, 0:256], self.psk(bank), [("ostg", l, 1)], eng="act")
                    if "c" in os.environ.get("LASTSEL", "abc"):
                        self.dma("sp", self.o["nkp"][l], stg[:, 0:128], "outm", [("ostg", l, 1)], [])
                        self.dma("sp", self.o["nvp"][l], stg[:, 128:256], "outm", [("ostg", l, 1)], [])
            E = [self.av(6 + w, 1) for w in range(2)]
            PT = self.av(8, 1, BF16).rearrange("p (w n) -> p w n", w=2)
            for b in range(4):
                first = (t == 0 and b == 0)
                for w in range(2):
                    if w == 0 and first:
                        continue
                    bank = self.ring_ps(4)
                    grp = []
                    for hs in range(4):
                        grp.append((self.ps[bank][:, hs * 128:(hs + 1) * 128], self.kTa[:, l, (b + w) * 128:(b + w + 1) * 128],
                                    qm[:, hs, b * 128:(b + 1) * 128], True, True))
                    import os
                    SUB = int(os.environ.get("SUB", "9"))
                    self.mm_group(grp, [kkey] + self.pg(4, 2), self.psk(bank))
                    if SUB >= 2:
                        self.act(E[w][:, :], self.ps[bank][:, :], AF.Exp, self.psk(bank), self.pg(6 + w))
                    if SUB >= 3:
                        self.tt(PT[:, w, :], E[w][:, :], self.EBp[:, w, :, :].rearrange("p h q -> p (h q)"), ALU.mult, self.pg(6 + w) + ["EBp"], self.pg(8))
                if SUB < 9:
                    continue
                ws = (1,) if first else (0, 1)
                for p in range(2):
                    seq = [(c, w) for c in range(2) for w in ws]
                    grp = []
                    for i, (c, w) in enumerate(seq):
                        grp.append((self.ps[4 + p][:, b * 128:(b + 1) * 128], self.vatm[:, l, b + w, c, :],
                                    PT[:, w, (p * 2 + c) * 128:(p * 2 + c + 1) * 128], i == 0, i == len(seq) - 1))
                    for i, (c, w) in enumerate(seq):
                        grp.append((self.ps[6 + p][:, b * 128:(b + 1) * 128], self.onespad[:, c, :],
                                    PT[:, w, (p * 2 + c) * 128:(p * 2 + c + 1) * 128], i == 0, i == len(seq) - 1))
                    self.mm_group(grp, [vkey, "onespad"] + self.pg(8), [("ps", 4 + p), ("ps", 6 + p)])
            den = self.av(9, 1)
            ACUT = 9
            if ACUT <= 3:
                for k in range(6, 8):
                    P.add("dve", lambda e, k=k: e.memset(mix[:, k, :], 0.0), [], self.pg(k // 2))
            import os
            if int(os.environ.get("SUB", "9")) < 9:
                for k in range(6, 8):
                    P.add("dve", lambda e, k=k: e.memset(mix[:, k, :], 0.0), [], self.pg(k // 2))
            for p in range(2 if int(os.environ.get("SUB", "9")) >= 9 else 0):
                self.ts(den[:, :], self.ps[6 + p][:, :], self.esink[:, l * 2 + p:l * 2 + p + 1], ALU.add, self.psk(6 + p) + ["esink"], self.pg(9))
                P.add("dve", lambda e: e.reciprocal(out=den[:, :], in_=den[:, :]), self.pg(9), self.pg(9))
                self.tt(mix[:, 6 + p, :], self.ps[4 + p][:, :], den[:, :], ALU.mult, self.psk(4 + p) + self.pg(9), self.pg(3))
            self.copy(self.kTa[:, l, 0:128], self.kTa[:, l, N:N + 128], [kkey], [kkey])
            self.copy(self.vatm[:, l, 0, :, :], self.vatm[:, l, 4, :, :], [vkey], [vkey])
        if self.stage == 4:
            for k in range(2, 6):
                P.add("dve", lambda e, k=k: e.memset(mix[:, k, :], 0.0), [], self.pg(k // 2))
        if self.stage >= 5 and HGRN_ON:
            self.hgrn_prompt(l, s0, s1, s2, last)
        elif self.stage >= 5:
            for k in range(2, 6):
                P.add("dve", lambda e, k=k: e.memset(mix[:, k, :], 0.0), [], self.pg(k // 2))
        for _ in range(3):
            self.w_release()
        s3 = self.w_acquire()
        wv3 = self.wring[s3][:, :].rearrange("p (k n) -> p k n", k=8)
        for j in range(8):
            bank = self.ring_ps(8)
            self.mm_group([(self.ps[bank][:, 0:N], wv3[:, k, j * 128:(j + 1) * 128], mix[:, k, 0:N], k == 0, k == 7) for k in range(8)],
                          [("w", s3)] + self.pg(0, 4), self.psk(bank))
            self.gated_residual(N, l, j, bank, 16, "p")
        self.w_release()

    def hgrn_prompt(self, l, s0, s1, s2, last):
        N = NT
        VR = VROWS_PER_LAYER
        P = self.P
        h = self.h
        mix = self.av(0, 4, BF16).rearrange("p (k n) -> p k n", k=8)
        wv1 = self.wring[s1][:, :].rearrange("p (k n) -> p k n", k=8)
        vtm = self.av(14, 2, BF16).rearrange("p (b n) -> p b n", b=4)
        for b in range(4):
            bank = self.ring_ps(4)
            self.mm_group([(self.ps[bank][:, :], h[:, k, b * 128:(b + 1) * 128], wv1[:, k, 512:1024], k == 0, k == 7) for k in range(8)],
                          [("w", s1)] + self.hkeys(), self.psk(bank))
            self.copy(vtm[:, b, :], self.ps[bank][:, :], self.psk(bank), self.pg(14 + b // 2), eng="act")
        skey = f"S{l}"
        sig, lf, kraw, brel, d1, eq, ek, ktf = [self.av(4 + i, 1) for i in range(8)]
        qt = self.av(12, 1, BF16)[:, 0:512]
        ktb = self.av(12, 1, BF16)[:, 512:1024]
        sgate = self.av(13, 1)
        osq = self.av(16, 1, BF16)[:, 0:512]
        ktm = self.av(16, 1, BF16)[:, 512:1024].rearrange("p (b n) -> p b n", b=4)
        rr = self.av(17, 1)
        t1 = self.av(18, 1)
        Sbf = self.av(19, 2, BF16).rearrange("p (c n) -> p c n", c=16)
        chs = self.chs
        for hh in range(4):
            bq = self.fm_proj(s0, 512 + hh * 128, N)
            bf = self.fm_proj(s1, hh * 128, N)
            bg = self.fm_proj(s2, hh * 128, N)
            lbc = self.lbv[:, l, 0, hh:hh + 1]
            omlc = self.lbv[:, l, 1, hh:hh + 1]
            nomlc = self.lbv[:, l, 2, hh:hh + 1]
            import os
            HS = int(os.environ.get("HS", "9"))
            if HS >= 2:
                self.act(sig[:, :], self.ps[bf][:, :], AF.Sigmoid, self.psk(bf), self.pg(4))
                self.ts(lf[:, :], sig[:, :], omlc, ALU.mult, self.pg(4) + ["lbv"], self.pg(5), lbc, ALU.add)
                self.act(lf[:, :], lf[:, :], AF.Ln, self.pg(5), self.pg(5))
                self.ts(kraw[:, :], sig[:, :], nomlc, ALU.mult, self.pg(4) + ["lbv"], self.pg(6), omlc, ALU.add)
            if HS >= 3:
                P.add("dve", lambda e: e.tensor_tensor_scan(out=brel[:, :], data0=self.rstp[:, :], data1=lf[:, :], initial=0.0,
                                                            op0=ALU.mult, op1=ALU.add), self.pg(5) + ["rstp"], self.pg(7))
            br3 = brel[:, :].rearrange("p (c n) -> p c n", c=16)
            mid = br3[:, :, 15]
            bL = br3[:, :, 31]
            if HS >= 4:
                self.tt(d1[:, :].rearrange("p (c n) -> p c n", c=16), br3, bcast(mid, 32), ALU.subtract, self.pg(7), self.pg(8))
                self.act(eq[:, :], d1[:, :], AF.Exp, self.pg(8), self.pg(9))
                self.act(ek[:, :], d1[:, :], AF.Exp, self.pg(8), self.pg(10), scale=-1.0)
                self.tt(qt, self.ps[bq][:, :], eq[:, :], ALU.mult, self.psk(bq) + self.pg(9), self.pg(12))
                self.tt(ktf[:, :], kraw[:, :], ek[:, :], ALU.mult, self.pg(6) + self.pg(10), self.pg(11))
                self.copy(ktb, ktf[:, :], self.pg(11), self.pg(12), eng="act")
            if HS >= 5:
                self.act(chs[:, 0, :], mid, AF.Exp, self.pg(7), ["chs"])
                self.act(chs[:, 1, :], bL, AF.Exp, self.pg(7), ["chs"])
                self.tt(chs[:, 3, :], bL, mid, ALU.subtract, self.pg(7), ["chs"])
                self.act(chs[:, 2, :], chs[:, 3, :], AF.Exp, ["chs"], ["chs"])
            if HS >= 6:
                self.act(sgate[:, :], self.ps[bg][:, :], AF.Silu, self.psk(bg), self.pg(13))

            import os
            HC = int(os.environ.get("HC", "9"))
            if HC <= 1:
                P.add("dve", lambda e, hh=hh: e.memset(mix[:, 2 + hh, :], 0.0), [], self.pg(1 + hh // 2))
                continue
            bankT = self.ring_ps(4)
            for b in range(4):
                self.transpose_f32(self.ps[bankT][:, b * 128:(b + 1) * 128], ktf[:, b * 128:(b + 1) * 128], 128, self.pg(11), self.psk(bankT))
            kx = self.av(21, 2, BF16)
            pT = self.ps[bankT][:, :]
            pa = [list(x) for x in pT.ap]
            in0 = bass.AP(tensor=pT.tensor, offset=pT.offset, ap=[pa[0], [128, 4], [0, 4], [1, 128]])
            cm = self.cmask[:, :]
            ca = [list(x) for x in cm.ap]
            in1 = bass.AP(tensor=cm.tensor, offset=cm.offset, ap=[ca[0], [0, 4], [1, 4], [0, 128]])
            kx4 = kx.rearrange("p (b c n) -> p b c n", b=4, c=4)
            self.tt(kx4, in0, in1, ALU.mult, self.psk(bankT) + ["cmask"], self.pg(21, 2))
            bU = [self.ring_ps(4) for _ in range(4)]
            for q4 in range(4):
                grp = []
                for cc in range(4):
                    grp.append((self.ps[bU[q4]][:, cc * 128:(cc + 1) * 128], kx4[:, q4, cc, :],
                                vtm[:, q4, hh * 128:(hh + 1) * 128], True, True))
                self.mm_group(grp, self.pg(21, 2) + self.pg(14, 2), self.psk(bU[q4]))
            if HC <= 2:
                P.add("dve", lambda e, hh=hh: e.memset(mix[:, 2 + hh, :], 0.0), [], self.pg(1 + hh // 2))
                continue
            Sh = self.S[:, l, hh, :]
            for c in range(16):
                self.ts(Sbf[:, c, :], Sh, chs[:, 0, c:c + 1], ALU.mult, [skey, "chs"], self.pg(19 + c // 8))
                wt = self.wt[:, c % 2, :]
                self.ts(wt, self.ps[bU[c // 4]][:, (c % 4) * 128:(c % 4 + 1) * 128], chs[:, 2, c:c + 1], ALU.mult,
                        self.psk(bU[c // 4]) + ["chs"], [("wt", c % 2)])
                self.stt(Sh, Sh, chs[:, 1, c:c + 1], wt, ALU.mult, ALU.add, [skey, "chs", ("wt", c % 2)], [skey])
            bankA = self.ring_ps(4)
            self.mm_group([(self.ps[bankA][:, b * 128:(b + 1) * 128], ktb[:, b * 128:(b + 1) * 128], qt[:, b * 128:(b + 1) * 128], True, True)
                           for b in range(4)], self.pg(12), self.psk(bankA))
            self.tt(self.am[:, :, :], self.ps[bankA][:, :].rearrange("p (b n) -> p b n", b=4), bcast_mid(self.mAp[:, :], 4), ALU.mult,
                    self.psk(bankA) + ["mAp"], ["am"])
            if HC <= 3:
                P.add("dve", lambda e, hh=hh: e.memset(mix[:, 2 + hh, :], 0.0), [], self.pg(1 + hh // 2))
                continue
            ob = 4 + hh
            grp = []
            for b in range(4):
                grp.append((self.ps[ob][:, b * 128:(b + 1) * 128], vtm[:, b, hh * 128:(hh + 1) * 128], self.am[:, b, :], True, False))
                for c in range(4 * b, 4 * b + 4):
                    grp.append((self.ps[ob][:, c * 32:(c + 1) * 32], Sbf[:, c, :], qt[:, c * 32:(c + 1) * 32], False, c == 4 * b + 3))
            self.mm_group(grp, self.pg(14, 2) + ["am"] + self.pg(19, 2) + self.pg(12), self.psk(ob))
            self.act(osq, self.ps[ob][:, :], AF.Square, self.psk(ob), self.pg(16))
            bs = self.ring_ps(4)
            self.mm_group([(self.ps[bs][:, :], self.onesb[:, :], osq, True, True)], ["onesb"] + self.pg(16), self.psk(bs))
            self.act(rr[:, :], self.ps[bs][:, :], AF.Ln, self.psk(bs) + ["epsc"], self.pg(17), bias=self.epsc[:, 0:1], scale=1.0 / 128.0)
            self.act(rr[:, :], rr[:, :], AF.Exp, self.pg(17), self.pg(17), scale=-0.5)
            self.tt(t1[:, :], self.ps[ob][:, :], rr[:, :], ALU.mult, self.psk(ob) + self.pg(17), self.pg(18))
            self.stt(mix[:, 2 + hh, :], t1[:, :], self.vecT[:, l * VR + V_HG:l * VR + V_HG + 1], sgate[:, :], ALU.mult, ALU.mult,
                     self.pg(18) + self.pg(13) + ["vecT"], self.pg(1 + hh // 2))
        import os
        if last and "b" in os.environ.get("LASTSEL", "abc"):
            self.dma("sp", self.o["nhp"][l].rearrange("h d v -> d h v"), self.S[:, l, :, :], "outm", [skey], [])

    def tile_sample(self):
        N = NSS * TS
        self.load_x(self.d["xs"], 1, N)
        for l in range(DEPTH):
            self.mixer_sample(l)
            self.mlp(N, l, "s")
        self.final_store(N, self.o["ys"], 1, N, "s")

    def mixer_sample(self, l):
        N = NSS * TS
        VR = VROWS_PER_LAYER
        P, d, o = self.P, self.d, self.o
        h = self.h
        mix = self.av(0, 4, BF16).rearrange("p (k n) -> p k n", k=8)
        self.rmsnorm_mod(N, l, 8, 0, None, None, "s")
        s0, s1, s2 = self.w_acquire(), self.w_acquire(), self.w_acquire()
        wv1 = self.wring[s1][:, :].rearrange("p (k n) -> p k n", k=8)
        wv2 = self.wring[s2][:, :].rearrange("p (k n) -> p k n", k=8)
        cc3 = d["cc"][l].rearrange("(s r) c -> s r c", r=30)
        self.dma("sp", o["ncs"][l][:, 0:26, :], cc3[:, 4:30, :], "outm", [], [])
        for g in range(4):
            stg = self.av(15, 1)[0:120, 0:256]
            self.dma("sp", stg, d["cc"][l][g * 120:(g + 1) * 120, :], "cin", [], self.pg(15))
            for c in range(2):
                bank = self.ring_ps(4)
                self.transpose_f32(self.ps[bank][:, 0:120], stg[:, c * 128:(c + 1) * 128], 120, self.pg(15), self.psk(bank))
                self.copy(self.a_exts[:, c, 4 * g:4 * g + 4, 0:30], self.ps[bank][:, 0:120].rearrange("p (s r) -> p s r", r=30),
                          self.psk(bank), ["a_exts"])
        a32 = [self.av(4 + c, 1)[:, 0:N] for c in range(2)]

        def aext_view(c, k):
            if k is None:
                return self.a_exts[:, c, :, 30:30 + TS]
            return self.a_exts[:, c, :, k:k + TS]
        self.conv_branch(N, l, s0, aext_view, "a_exts", a32)
        bank = self.ring_ps(4)
        for c in range(2):
            self.P.add("pe", lambda e, c=c, bank=bank: e.matmul(self.ps[bank][0:N, c * 128:(c + 1) * 128], lhsT=a32[c], rhs=self.ident[:, :],
                                                                start=True, stop=True), self.pg(4 + c) + ["ident"], self.psk(bank))
        self.copy(self.ostgs[:, l, 0, :], self.ps[bank][0:N, 0:256], self.psk(bank), [("ostgs", l, 0)])
        for sq in range(NSS):
            self.dma("sp", o["ncs"][l][sq, 26:30, :], self.ostgs[TS * sq:TS * sq + TS, l, 0, :], "outm", [("ostgs", l, 0)], [])
        qm = self.av(4, 1, BF16)[:, 0:4 * N].rearrange("p (k n) -> p k n", k=4)
        kTn = self.av(5, 1, BF16)[:, 0:N]
        P.add("dve", lambda e: e.memset(qm, 0.0), [], self.pg(4))
        for p in range(2):
            bk = self.fm_proj(s2, 512 + p * 128, N)
            for c in range(2):
                self.act(qm[c * 64:(c + 1) * 64, p * 2 + c, :], self.ps[bk][c * 64:(c + 1) * 64, 0:N], AF.Identity, self.psk(bk), self.pg(4), scale=0.125)
        bk = self.fm_proj(s2, 768, N)
        self.copy(kTn, self.ps[bk][:, 0:N], self.psk(bk), self.pg(5), eng="act")
        bank = self.ring_ps(4)
        self.mm_group([(self.ps[bank][0:N, 0:256], h[:, k, 0:N], wv2[:, k, 768:1024], k == 0, k == 7) for k in range(8)],
                      [("w", s2)] + self.hkeys(), self.psk(bank))
        vnp = self.av(19, 1, BF16)[0:N, 0:256].rearrange("p (c n) -> p c n", c=2)
        P.add("dve", lambda e: e.memset(vnp, 0.0), [], self.pg(19))
        for c in range(2):
            self.copy(vnp[:, c, c * 64:(c + 1) * 64], self.ps[bank][0:N, 128 + c * 64:128 + (c + 1) * 64], self.psk(bank), self.pg(19))
        self.copy(self.ostgs[:, l, 1, :], self.ps[bank][0:N, 0:256], self.psk(bank), [("ostgs", l, 1)])
        self.dma("sp", o["nks"][l][:, 0:124, :], d["ck"][l][:, 4:128, :], "outm", [], [])
        self.dma("sp", o["nvs"][l][:, 0:124, :], d["cv"][l][:, 4:128, :], "outm", [], [])
        for sq in range(NSS):
            self.dma("sp", o["nks"][l][sq, 124:128, :], self.ostgs[TS * sq:TS * sq + TS, l, 1, 0:128], "outm", [("ostgs", l, 1)], [])
            self.dma("sp", o["nvs"][l][sq, 124:128, :], self.ostgs[TS * sq:TS * sq + TS, l, 1, 128:256], "outm", [("ostgs", l, 1)], [])
        c32 = self.av(6, 4).rearrange("p (s n) -> p s n", s=NSS)
        kcT = self.av(10, 2, BF16).rearrange("p (s n) -> p s n", s=NSS)
        vcp = self.av(12, 4, BF16).rearrange("p (s c n) -> p s c n", s=NSS, c=2)
        self.dma("sp", c32, d["ck"][l].rearrange("s k f -> k s f"), "cin", [], self.pg(6, 4))
        for g in range(4):
            bank = self.ring_ps(4)
            for i in range(4):
                self.transpose_f32(self.ps[bank][:, i * 128:(i + 1) * 128], c32[:, 4 * g + i, :], 128, self.pg(6, 4), self.psk(bank))
            self.copy(kcT[:, 4 * g:4 * g + 4, :], self.ps[bank][:, :].rearrange("p (s n) -> p s n", s=4), self.psk(bank), self.pg(10, 2),
                      eng=("act" if g % 2 else "dve"))
        self.dma("sp", c32, d["cv"][l].rearrange("s k f -> k s f"), "cin", [], self.pg(6, 4))
        P.add("dve", lambda e: e.memset(vcp, 0.0), [], self.pg(12, 4))
        for c in range(2):
            self.copy(vcp[:, :, c, c * 64:(c + 1) * 64], c32[:, :, c * 64:(c + 1) * 64], self.pg(6, 4), self.pg(12, 4), eng=("act" if c else "dve"))
        bank = self.ring_ps(4)
        grp = []
        for hs in range(4):
            for sq in range(NSS):
                grp.append((self.ps[bank][:, hs * N + TS * sq:hs * N + TS * sq + TS], kcT[:, sq, :], qm[:, hs, TS * sq:TS * sq + TS], True, True))
        self.mm_group(grp, self.pg(10, 2) + self.pg(4), self.psk(bank))
        E = self.av(16, 1)[:, 0:4 * N]
        PTc = self.av(17, 1, BF16)[:, 0:4 * N].rearrange("p (k n) -> p k n", k=4)
        self.act(E, self.ps[bank][:, 0:4 * N], AF.Exp, self.psk(bank), self.pg(16))
        eb = self.EBsc[:, :, :]
        ea = [list(x) for x in eb.ap]
        eb_b = bass.AP(tensor=eb.tensor, offset=eb.offset, ap=[ea[0], [TS, 4], [0, NSS], [1, TS]])
        self.tt(PTc.rearrange("p k (s t) -> p k s t", t=TS), E.rearrange("p (k s t) -> p k s t", k=4, t=TS), eb_b, ALU.mult,
                self.pg(16) + ["EBsc"], self.pg(17))
        bank2 = self.ring_ps(4)
        self.mm_group([(self.ps[bank2][0:N, hs * N:(hs + 1) * N], kTn, qm[:, hs, :], True, True) for hs in range(4)],
                      self.pg(5) + self.pg(4), self.psk(bank2))
        E2 = self.av(18, 1)[0:N, 0:4 * N]
        PTn = self.av(17, 1, BF16)[0:N, 512:512 + 4 * N].rearrange("p (k n) -> p k n", k=4)
        self.act(E2, self.ps[bank2][0:N, 0:4 * N], AF.Exp, self.psk(bank2), self.pg(18))
        self.tt(PTn, E2.rearrange("p (k n) -> p k n", k=4), self.EBsn[:, :, :], ALU.mult, self.pg(18) + ["EBsn"], self.pg(17))
        for p in range(2):
            grp = []
            for c in range(2):
                grp.append((self.ps[4 + p][:, 0:N], vnp[:, c, :], PTn[:, p * 2 + c, :], c == 0, False))
            for sq in range(NSS):
                for c in range(2):
                    grp.append((self.ps[4 + p][:, TS * sq:TS * sq + TS], vcp[:, sq, c, :], PTc[:, p * 2 + c, TS * sq:TS * sq + TS], False,
                                sq == NSS - 1 and c == 1))
            seq = [("c", 0), ("c", 1), ("n", 0), ("n", 1)]
            for i, (kind, c) in enumerate(seq):
                if kind == "c":
                    grp.append((self.ps[6 + p][:, 0:N], self.onespad[:, c, :], PTc[:, p * 2 + c, :], i == 0, False))
                else:
                    grp.append((self.ps[6 + p][:, 0:N], self.onespad[0:N, c, :], PTn[:, p * 2 + c, :], False, i == 3))
            self.mm_group(grp, self.pg(12, 4) + self.pg(17) + self.pg(19) + ["onespad"], [("ps", 4 + p), ("ps", 6 + p)])
        den = self.av(20, 1)[:, 0:N]
        for p in range(2):
            self.ts(den, self.ps[6 + p][:, 0:N], self.esink[:, l * 2 + p:l * 2 + p + 1], ALU.add, self.psk(6 + p) + ["esink"], self.pg(20))
            P.add("dve", lambda e: e.reciprocal(out=den, in_=den), self.pg(20), self.pg(20))
            self.tt(mix[:, 6 + p, 0:N], self.ps[4 + p][:, 0:N], den, ALU.mult, self.psk(4 + p) + self.pg(20), self.pg(3))
        vtm = self.av(14, 1, BF16)[0:N, 0:512]
        bank = self.ring_ps(4)
        self.mm_group([(self.ps[bank][0:N, :], h[:, k, 0:N], wv1[:, k, 512:1024], k == 0, k == 7) for k in range(8)],
                      [("w", s1)] + self.hkeys(), self.psk(bank))
        self.copy(vtm, self.ps[bank][0:N, :], self.psk(bank), self.pg(14), eng="act")
        sig, lf, kraw, brel, d1, eq, ek, ktf = [self.av(4 + i, 1)[:, 0:N] for i in range(8)]
        qt = self.av(12, 1, BF16)[:, 0:N]
        ktb = self.av(12, 1, BF16)[:, 512:512 + N]
        sgate = self.av(13, 1)[:, 0:N]
        osq = self.av(14, 1, BF16)[:, 512:512 + N]
        rr = self.av(8, 1)[:, 0:N]
        t1 = self.av(9, 1)[:, 0:N]
        t2 = self.av(10, 1)
        S0 = self.av(15, 4).rearrange("p (s n) -> p s n", s=NSS)
        Sbf = self.av(19, 2, BF16).rearrange("p (s n) -> p s n", s=NSS)
        kx = self.av(21, 2, BF16)[0:N, :].rearrange("p (s n) -> p s n", s=NSS)
        ams = self.am[0:N, 0, 0:N]
        chs = self.chs
        for hh in range(4):
            bq = self.fm_proj(s0, 512 + hh * 128, N)
            bf = self.fm_proj(s1, hh * 128, N)
            bg = self.fm_proj(s2, hh * 128, N)
            lbc = self.lbv[:, l, 0, hh:hh + 1]
            omlc = self.lbv[:, l, 1, hh:hh + 1]
            nomlc = self.lbv[:, l, 2, hh:hh + 1]
            self.dma("sp", S0, d["sh"][l][:, hh].rearrange("s d v -> d s v"), "cin", [], self.pg(15, 4))
            self.act(sig, self.ps[bf][:, 0:N], AF.Sigmoid, self.psk(bf), self.pg(4))
            self.ts(lf, sig, omlc, ALU.mult, self.pg(4) + ["lbv"], self.pg(5), lbc, ALU.add)
            self.act(lf, lf, AF.Ln, self.pg(5), self.pg(5))
            self.ts(kraw, sig, nomlc, ALU.mult, self.pg(4) + ["lbv"], self.pg(6), omlc, ALU.add)
            P.add("dve", lambda e: e.tensor_tensor_scan(out=brel, data0=self.rsts[:, :], data1=lf, initial=0.0,
                                                        op0=ALU.mult, op1=ALU.add), self.pg(5) + ["rsts"], self.pg(7))
            br3 = brel.rearrange("p (c n) -> p c n", n=TS)
            mid = br3[:, :, 1]
            bL = br3[:, :, 3]
            self.tt(d1.rearrange("p (c n) -> p c n", n=TS), br3, bcast(mid, TS), ALU.subtract, self.pg(7), self.pg(8))
            self.act(eq, d1, AF.Exp, self.pg(8), self.pg(9))
            self.act(ek, d1, AF.Exp, self.pg(8), self.pg(10), scale=-1.0)
            self.tt(qt, self.ps[bq][:, 0:N], eq, ALU.mult, self.psk(bq) + self.pg(9), self.pg(12))
            self.tt(ktf, kraw, ek, ALU.mult, self.pg(6) + self.pg(10), self.pg(11))
            self.copy(ktb, ktf, self.pg(11), self.pg(12), eng="act")
            self.act(chs[:, 0, :], mid, AF.Exp, self.pg(7), ["chs"])
            self.act(chs[:, 1, :], bL, AF.Exp, self.pg(7), ["chs"])
            self.tt(chs[:, 3, :], bL, mid, ALU.subtract, self.pg(7), ["chs"])
            self.act(chs[:, 2, :], chs[:, 3, :], AF.Exp, ["chs"], ["chs"])
            self.act(sgate, self.ps[bg][:, 0:N], AF.Silu, self.psk(bg), self.pg(13))
            bankT = self.ring_ps(4)
            self.P.add("pe", lambda e, bankT=bankT: e.matmul(self.ps[bankT][0:N, 0:128], lhsT=ktf, rhs=self.ident[:, :], start=True, stop=True),
                       self.pg(11) + ["ident"], self.psk(bankT))
            pT = self.ps[bankT][0:N, 0:128]
            pa = [list(x) for x in pT.ap]
            in0 = bass.AP(tensor=pT.tensor, offset=pT.offset, ap=[pa[0], [0, NSS], [1, 128]])
            self.tt(kx, in0, bcast(self.seqoh[:, :], 128), ALU.mult, self.psk(bankT) + ["seqoh"], self.pg(21, 2))
            bU = [self.ring_ps(4) for _ in range(4)]
            for q4 in range(4):
                self.mm_group([(self.ps[bU[q4]][:, i * 128:(i + 1) * 128], kx[:, 4 * q4 + i, :], vtm[:, hh * 128:(hh + 1) * 128], True, True)
                               for i in range(4)], self.pg(21, 2) + self.pg(14), self.psk(bU[q4]))
            self.tt(Sbf, S0, bcast(chs[:, 0, :], 128), ALU.mult, self.pg(15, 4) + ["chs"], self.pg(19, 2))
            self.tt(S0, S0, bcast(chs[:, 1, :], 128), ALU.mult, self.pg(15, 4) + ["chs"], self.pg(15, 4))
            for q4 in range(4):
                t23 = t2.rearrange("p (s n) -> p s n", s=4)
                self.tt(t23, self.ps[bU[q4]][:, :].rearrange("p (s n) -> p s n", s=4), bcast(chs[:, 2, 4 * q4:4 * q4 + 4], 128), ALU.mult,
                        self.psk(bU[q4]) + ["chs"], self.pg(10))
                self.tt(S0[:, 4 * q4:4 * q4 + 4, :], S0[:, 4 * q4:4 * q4 + 4, :], t23, ALU.add, self.pg(15, 4) + self.pg(10), self.pg(15, 4))
            self.dma("sp", o["nhs"][l][:, hh].rearrange("s d v -> d s v"), S0, "outh", self.pg(15, 4), [])
            bankA = self.ring_ps(4)
            self.mm_group([(self.ps[bankA][0:N, 0:N], ktb, qt, True, True)], self.pg(12), self.psk(bankA))
            self.tt(ams, self.ps[bankA][0:N, 0:N], self.mAs[:, :], ALU.mult, self.psk(bankA) + ["mAs"], ["am"])
            ob = 4 + hh
            grp = [(self.ps[ob][:, 0:N], vtm[:, hh * 128:(hh + 1) * 128], ams, True, False)]
            for sq in range(NSS):
                grp.append((self.ps[ob][:, TS * sq:TS * sq + TS], Sbf[:, sq, :], qt[:, TS * sq:TS * sq + TS], False, sq == NSS - 1))
            self.mm_group(grp, self.pg(14) + ["am"] + self.pg(19, 2) + self.pg(12), self.psk(ob))
            self.act(osq, self.ps[ob][:, 0:N], AF.Square, self.psk(ob), self.pg(14))
            bs = self.ring_ps(4)
            self.mm_group([(self.ps[bs][:, 0:N], self.onesb[:, :], osq, True, True)], ["onesb"] + self.pg(14), self.psk(bs))
            self.act(rr, self.ps[bs][:, 0:N], AF.Ln, self.psk(bs) + ["epsc"], self.pg(8), bias=self.epsc[:, 0:1], scale=1.0 / 128.0)
            self.act(rr, rr, AF.Exp, self.pg(8), self.pg(8), scale=-0.5)
            self.tt(t1, self.ps[ob][:, 0:N], rr, ALU.mult, self.psk(ob) + self.pg(8), self.pg(9))
            self.stt(mix[:, 2 + hh, 0:N], t1, self.vecT[:, l * VR + V_HG:l * VR + V_HG + 1], sgate, ALU.mult, ALU.mult,
                     self.pg(9) + self.pg(13) + ["vecT"], self.pg(1 + hh // 2))
        for _ in range(3):
            self.w_release()
        s3 = self.w_acquire()
        wv3 = self.wring[s3][:, :].rearrange("p (k n) -> p k n", k=8)
        for j in range(8):
            bank = self.ring_ps(8)
            self.mm_group([(self.ps[bank][:, 0:N], wv3[:, k, j * 128:(j + 1) * 128], mix[:, k, 0:N], k == 0, k == 7) for k in range(8)],
                          [("w", s3)] + self.pg(0, 4), self.psk(bank))
            self.gated_residual(N, l, j, bank, 16, "s")
        self.w_release()

    def tile_prompt(self, t, last):
        self.load_x(self.d["xp"][t * NT:(t + 1) * NT, :], 4, 128)
        for l in range(DEPTH):
            if self.stage >= 3:
                self.mixer_prompt(t, l, last)
            else:
                for _ in range(4):
                    self.w_acquire()
                    self.w_release()
            if self.stage >= 2:
                self.mlp(NT, l, "p")
            else:
                for _ in range(8):
                    self.w_acquire()
                    self.w_release()
        self.final_store(NT, self.o["yp"][t * NT:(t + 1) * NT, :], 4, 128, "p")


_CACHE = {}


def _get_nc(stage=60):
    if stage not in _CACHE:
        b = Builder(stage)
        _CACHE[stage] = b.build()
    return _CACHE[stage]


def _pack_vecs(inp):
    rows = []
    for l in range(DEPTH):
        rows.append(inp["b_ada"][l].reshape(48, 128))
        rows.append(inp["norm_mix_g"][l].reshape(8, 128))
        rows.append(inp["norm_mlp_g"][l].reshape(8, 128))
        rows.append(inp["conv_b"][l].reshape(2, 128))
        rows.append(inp["conv_ln_g"][l].reshape(2, 128))
        rows.append(inp["conv_ln_b"][l].reshape(2, 128))
        rows.append(inp["hgrn_lb"][l].reshape(4, 128))
        rows.append(inp["hgrn_norm_g"][l].reshape(1, 128))
        rows.append(inp["conv_w"][l].reshape(31 * 2, 128))
    rows.append(inp["final_g"].reshape(8, 128))
    v = np.ascontiguousarray(np.concatenate(rows, axis=0).astype(np.float32))
    assert v.shape == (NVROWS, 128)
    return v


def make_in_maps(inp, ncores=NCORES):
    f = lambda a: np.ascontiguousarray(np.asarray(a, dtype=np.float32))
    inp = {k: f(v) for k, v in inp.items()}
    st = _static_tables()
    vecs = _pack_vecs(inp)
    relb = np.ascontiguousarray(np.broadcast_to(inp["rel_bias"].reshape(1, 128), (128, 128)))
    sinkc = np.zeros((128, DEPTH * 2), np.float32)
    for l in range(DEPTH):
        for p in range(2):
            sinkc[0:64, l * 2 + p] = inp["attn_sinks"][l, p]
            sinkc[64:128, l * 2 + p] = inp["attn_sinks"][l, 2 + p]
    maps = []
    for c in range(ncores):
        b = c % 4
        ss = slice(c * NSS, (c + 1) * NSS)
        m = {
            "xp": inp["x_prompt"][b],
            "xs": inp["x_sample"][ss].reshape(NSS * TS, D),
            "cc": inp["cache_conv"][:, ss].reshape(DEPTH, NSS * 30, 256),
            "sh": inp["state_hgrn"][:, ss],
            "ck": inp["cache_swa_k"][:, ss].reshape(DEPTH, NSS, 128, 128),
            "cv": inp["cache_swa_v"][:, ss].reshape(DEPTH, NSS, 128, 128),
            "cvec": np.concatenate([inp["c_prompt"][b:b + 1], inp["c_sample"][ss]], axis=0),
            "relb": relb, "sinkc": sinkc, "vecs": vecs,
            "w_ada": inp["w_ada"], "w_in": inp["w_in"], "w_out": inp["w_out"], "w_up": inp["w_up"], "w_down": inp["w_down"],
        }
        for k in ("ident", "ohp", "ohsc", "ohsn", "mAp", "mAs", "rstp", "rsts", "seqoh", "cmask"):
            a = st[k]
            m[k] = np.ascontiguousarray(a.reshape(a.shape[0], -1))
        maps.append({k: np.ascontiguousarray(v) for k, v in m.items()})
    return maps


def kernel(**inputs):
    nc = _get_nc()
    maps = make_in_maps(inputs)
    res = run_bass_kernel_spmd(nc, maps, core_ids=list(range(NCORES)))
    r = res.results
    B = 4
    y_prompt = np.stack([r[b]["yp"] for b in range(B)]).reshape(B, SEQ, D)
    y_sample = np.concatenate([r[c]["ys"].reshape(NSS, TS, D) for c in range(NCORES)], axis=0)
    ncp = np.stack([r[b]["ncp"] for b in range(B)], axis=1)
    ncs = np.concatenate([r[c]["ncs"] for c in range(NCORES)], axis=1)
    nhp = np.stack([r[b]["nhp"] for b in range(B)], axis=1)
    nhs = np.concatenate([r[c]["nhs"] for c in range(NCORES)], axis=1)
    nkp = np.stack([r[b]["nkp"] for b in range(B)], axis=1).reshape(DEPTH, B, 128, 2, 64)
    nks = np.concatenate([r[c]["nks"] for c in range(NCORES)], axis=1).reshape(DEPTH, NCORES * NSS, 128, 2, 64)
    nvp = np.stack([r[b]["nvp"] for b in range(B)], axis=1).reshape(DEPTH, B, 128, 2, 64)
    nvs = np.concatenate([r[c]["nvs"] for c in range(NCORES)], axis=1).reshape(DEPTH, NCORES * NSS, 128, 2, 64)
    outs = (y_prompt, y_sample, ncp, ncs, nhp, nhs, nkp, nks, nvp, nvs)
    return tuple(np.ascontiguousarray(a.astype(np.float32)) for a in outs)
```
